# Optimizing a Trainium2 kernel written in Bass

```python
import math
import jax, jax.numpy as jnp
from jax import lax
import numpy as np

D_MODEL = 1024
BATCH = 2
SEQ = 16384
DEPTH = 2

N_META = 16
N_MIXERS = 2
N_A_LAYERS = (DEPTH + 1) // 2
N_B_LAYERS = DEPTH // 2
SC_WIDTH = 3
D_RNN = 1280
RG_BLOCKS = 10
RG_BLOCK_DIM = D_RNN // RG_BLOCKS
RG_CONV_WIDTH = 4
RG_C = 8.0
D_FF = 2816
FFN_CONV_WIDTH = 3
RMS_EPS = 1e-6

kernel_name = "hybrid_shortconv_rglru_convffn"


def rms_norm(x, g):
    xf = x.astype(jnp.float32)
    var = jnp.mean(xf * xf, axis=-1, keepdims=True)
    return (xf * lax.rsqrt(var + RMS_EPS) * g.astype(jnp.float32)).astype(x.dtype)


def causal_dwconv(x, w):
    k_width = w.shape[0]
    t_len = x.shape[1]
    xp = jnp.pad(x, ((0, 0), (k_width - 1, 0), (0, 0)))
    y = xp[:, 0:t_len] * w[0]
    for k in range(1, k_width):
        y = y + xp[:, k:k + t_len] * w[k]
    return y


def short_conv_mixer(x, w_in, conv_w, w_out):
    h = jnp.einsum('btd,de->bte', x, w_in)
    b_gate, c_gate, v = jnp.split(h, 3, axis=-1)
    u = causal_dwconv(c_gate * v, conv_w)
    return jnp.einsum('btd,de->bte', b_gate * u, w_out)


def _lin_rec_combine(left, right):
    a_l, b_l = left
    a_r, b_r = right
    return a_l * a_r, a_r * b_l + b_r


def rglru_block(x, w_in, conv_w, conv_b, w_gate_a, b_gate_a, w_gate_x, b_gate_x, lam, w_out):
    bsz, t_len, _ = x.shape
    h = jnp.einsum('btd,de->bte', x, w_in)
    g_branch, r_branch = jnp.split(h, 2, axis=-1)
    gate = jax.nn.gelu(g_branch, approximate=True)
    u = causal_dwconv(r_branch, conv_w) + conv_b
    ub = u.reshape(bsz, t_len, RG_BLOCKS, RG_BLOCK_DIM)
    r = jax.nn.sigmoid(jnp.einsum('btki,kij->btkj', ub, w_gate_a).reshape(bsz, t_len, D_RNN) + b_gate_a)
    i = jax.nn.sigmoid(jnp.einsum('btki,kij->btkj', ub, w_gate_x).reshape(bsz, t_len, D_RNN) + b_gate_x)
    log_a = -RG_C * r.astype(jnp.float32) * jax.nn.softplus(-lam.astype(jnp.float32))
    a = jnp.exp(log_a)
    mult = jnp.sqrt(-jnp.expm1(2.0 * log_a))
    b = mult * (i * u).astype(jnp.float32)
    _, hs = lax.associative_scan(_lin_rec_combine, (a, b), axis=1)
    y = hs.astype(x.dtype) * gate
    return jnp.einsum('bte,ed->btd', y, w_out)


def conv_gated_mlp(x, w_up, conv_w, w_down):
    h = jnp.einsum('btd,df->btf', x, w_up)
    h = causal_dwconv(h, conv_w)
    g, v = jnp.split(h, 2, axis=-1)
    return jnp.einsum('btf,fd->btd', jax.nn.silu(g) * v, w_down)


def setup_inputs(seed: int = 0) -> dict:
    key = jax.random.key(seed)
    ks = jax.random.split(key, 24)
    f32 = jnp.float32
    D = D_MODEL

    def nrm(k, shape, scale):
        return jax.random.normal(k, shape, f32) * scale

    x = jax.random.normal(ks[0], (BATCH, SEQ, D), f32)
    meta_tokens = nrm(ks[1], (N_META, D), 1.0)
    norm_mix_g = 1.0 + nrm(ks[2], (DEPTH, D), 0.01)
    norm_ffn_g = 1.0 + nrm(ks[3], (DEPTH, D), 0.01)
    final_norm_g = 1.0 + nrm(ks[4], (D,), 0.01)

    sc_w_in = nrm(ks[5], (N_A_LAYERS, D, 3 * D), D ** -0.5)
    sc_conv_w = nrm(ks[6], (N_A_LAYERS, SC_WIDTH, D), SC_WIDTH ** -0.5)
    sc_w_out = nrm(ks[7], (N_A_LAYERS, D, D), D ** -0.5)

    rg_w_in = nrm(ks[8], (N_B_LAYERS, D, 2 * D_RNN), D ** -0.5)
    rg_conv_w = nrm(ks[9], (N_B_LAYERS, RG_CONV_WIDTH, D_RNN), RG_CONV_WIDTH ** -0.5)
    rg_conv_b = nrm(ks[10], (N_B_LAYERS, D_RNN), 0.01)
    rg_w_gate_a = nrm(ks[11], (N_B_LAYERS, RG_BLOCKS, RG_BLOCK_DIM, RG_BLOCK_DIM), RG_BLOCK_DIM ** -0.5)
    rg_b_gate_a = nrm(ks[12], (N_B_LAYERS, D_RNN), 0.01)
    rg_w_gate_x = nrm(ks[13], (N_B_LAYERS, RG_BLOCKS, RG_BLOCK_DIM, RG_BLOCK_DIM), RG_BLOCK_DIM ** -0.5)
    rg_b_gate_x = nrm(ks[14], (N_B_LAYERS, D_RNN), 0.01)
    a_c = jax.random.uniform(ks[15], (N_B_LAYERS, D_RNN), f32, 0.9, 0.999)
    a_base = a_c ** (1.0 / RG_C)
    rg_lambda = jnp.log(a_base) - jnp.log1p(-a_base)
    rg_w_out = nrm(ks[16], (N_B_LAYERS, D_RNN, D), D_RNN ** -0.5)

    ffn_w_up = nrm(ks[17], (DEPTH, D, 2 * D_FF), D ** -0.5)
    ffn_conv_w = nrm(ks[18], (DEPTH, FFN_CONV_WIDTH, 2 * D_FF), FFN_CONV_WIDTH ** -0.5)
    ffn_w_down = nrm(ks[19], (DEPTH, D_FF, D), D_FF ** -0.5)

    return {"x": x, "meta_tokens": meta_tokens, "norm_mix_g": norm_mix_g,
            "norm_ffn_g": norm_ffn_g, "final_norm_g": final_norm_g,
            "sc_w_in": sc_w_in, "sc_conv_w": sc_conv_w, "sc_w_out": sc_w_out,
            "rg_w_in": rg_w_in, "rg_conv_w": rg_conv_w, "rg_conv_b": rg_conv_b,
            "rg_w_gate_a": rg_w_gate_a, "rg_b_gate_a": rg_b_gate_a,
            "rg_w_gate_x": rg_w_gate_x, "rg_b_gate_x": rg_b_gate_x,
            "rg_lambda": rg_lambda, "rg_w_out": rg_w_out,
            "ffn_w_up": ffn_w_up, "ffn_conv_w": ffn_conv_w, "ffn_w_down": ffn_w_down}


def reference(x, meta_tokens, norm_mix_g, norm_ffn_g, final_norm_g,
              sc_w_in, sc_conv_w, sc_w_out,
              rg_w_in, rg_conv_w, rg_conv_b, rg_w_gate_a, rg_b_gate_a,
              rg_w_gate_x, rg_b_gate_x, rg_lambda, rg_w_out,
              ffn_w_up, ffn_conv_w, ffn_w_down):
    bsz = x.shape[0]
    meta = jnp.broadcast_to(meta_tokens.astype(x.dtype)[None], (bsz, N_META, x.shape[-1]))
    h = jnp.concatenate([meta, x], axis=1)
    for layer in range(DEPTH):
        hn = rms_norm(h, norm_mix_g[layer])
        j = layer // N_MIXERS
        if layer % N_MIXERS == 0:
            mix = short_conv_mixer(hn, sc_w_in[j], sc_conv_w[j], sc_w_out[j])
        else:
            mix = rglru_block(hn, rg_w_in[j], rg_conv_w[j], rg_conv_b[j],
                              rg_w_gate_a[j], rg_b_gate_a[j], rg_w_gate_x[j], rg_b_gate_x[j],
                              rg_lambda[j], rg_w_out[j])
        h = h + mix
        h = h + conv_gated_mlp(rms_norm(h, norm_ffn_g[layer]), ffn_w_up[layer],
                               ffn_conv_w[layer], ffn_w_down[layer])
    out = rms_norm(h, final_norm_g)
    return out[:, N_META:]
```

```python
import contextlib
import numpy as np
import concourse.bass as bass
import concourse.mybir as mybir
from concourse.bass_utils import run_bass_kernel_spmd

F32 = mybir.dt.float32
BF16 = mybir.dt.bfloat16
AF = mybir.ActivationFunctionType
ALU = mybir.AluOpType

ENGS = ("pe", "act", "dve", "pool", "sp")

D = 1024
NCH = 8
NMETA = 16
SEQ = 16384
TNEW = 1024
HALO = 16
N = TNEW + HALO
PS_OFF = 248
SEGS = ((0, 512 - PS_OFF), (512 - PS_OFF, 1024 - PS_OFF), (1024 - PS_OFF, N))
NR = 4
DFF = 2816
NFF = 22
DRNN = 1280
NRB = 10
SCAN0 = 14
LPCOL = TNEW + SCAN0 - 1
EPS = 1e-6
WSLOT = 6144
NSLOT = 3
NTMP = 7


class Prog:
    def __init__(self, nc):
        self.nc = nc
        self.ops = {e: [] for e in ENGS}
        self.count = {e: 0 for e in ENGS}
        self.dma_count = {}
        self.known = {e: {} for e in ENGS}
        self.last_w = {}
        self.readers = {}

    def _deps(self, eng, reads, writes):
        need = {}

        def add(tok):
            if tok is not None and need.get(tok[0], 0) < tok[1]:
                need[tok[0]] = tok[1]

        for r in reads:
            add(self.last_w.get(r))
        for w in writes:
            add(self.last_w.get(w))
            for t in self.readers.get(w, ()):
                add(t)
        waits = []
        for k, v in need.items():
            if k == eng and eng == "pe":
                continue
            if self.known[eng].get(k, 0) >= v:
                continue
            self.known[eng][k] = v
            waits.append((k, v))
        return waits

    def _commit(self, tok, reads, writes):
        for r in reads:
            self.readers.setdefault(r, []).append(tok)
        for w in writes:
            self.last_w[w] = tok
            self.readers[w] = []

    def op(self, eng, fn, reads=(), writes=()):
        waits = self._deps(eng, reads, writes)
        self.count[eng] += 1
        tok = (eng, self.count[eng])
        self.ops[eng].append((waits, fn, (eng, 1)))
        self._commit(tok, reads, writes)

    def dma(self, queue, semkey, fn, reads=(), writes=()):
        waits = self._deps(queue, reads, writes)
        self.dma_count[semkey] = self.dma_count.get(semkey, 0) + 16
        tok = (semkey, self.dma_count[semkey])
        self.ops[queue].append((waits, fn, (semkey, 16)))
        self._commit(tok, reads, writes)

    def coll(self, semkey, fn, reads=(), writes=()):
        waits = self._deps("pool", reads, writes)
        self.dma_count[semkey] = self.dma_count.get(semkey, 0) + 1
        tok = (semkey, self.dma_count[semkey])
        self.ops["pool"].append((waits, fn, (semkey, 1)))
        self._commit(tok, reads, writes)

    def wait_all(self, eng, keys):
        waits = self._deps(eng, keys, ())
        self.ops[eng].append((waits, None, None))

    def emit(self):
        nc = self.nc
        semkeys = list(ENGS) + list(self.dma_count)
        with contextlib.ExitStack() as st:
            sems = {k: st.enter_context(nc.semaphore("s_" + str(k))) for k in semkeys}
            block = st.enter_context(nc.Block())

            def replay(e):
                def run(eng):
                    for waits, fn, inc in self.ops[e]:
                        for k, v in waits:
                            eng.wait_ge(sems[k], v)
                        if fn is None:
                            continue
                        ins = fn(eng)
                        if inc is not None:
                            ins.then_inc(sems[inc[0]], inc[1])
                return run

            block.tensor(replay("pe"))
            block.scalar(replay("act"))
            block.vector(replay("dve"))
            block.gpsimd(replay("pool"))
            block.sync(replay("sp"))


PP_FIELDS = (("g_mix0", 8), ("g_ffn0", 8), ("g_mix1", 8), ("g_ffn1", 8), ("g_fin", 8),
             ("sc_cw", 24), ("rg_cw", 40), ("rg_cb", 10), ("rg_ba", 10), ("rg_bx", 10),
             ("rg_lam", 10), ("ffn_cw0", 132), ("ffn_cw1", 132))
PP_OFF = {}
_o = 0
for _n, _w in PP_FIELDS:
    PP_OFF[_n] = _o
    _o += _w
NPP = _o


def _vec_pc(v):
    v = np.asarray(v, np.float32)
    return np.ascontiguousarray(v.reshape(-1, 128).T)


def _conv_pc(w):
    w = np.asarray(w, np.float32)
    K = w.shape[0]
    a = w.reshape(K, -1, 128)
    return np.ascontiguousarray(a.transpose(2, 1, 0).reshape(128, -1))


def pack_params(inp):
    parts = {
        "g_mix0": _vec_pc(inp["norm_mix_g"][0]), "g_ffn0": _vec_pc(inp["norm_ffn_g"][0]),
        "g_mix1": _vec_pc(inp["norm_mix_g"][1]), "g_ffn1": _vec_pc(inp["norm_ffn_g"][1]),
        "g_fin": _vec_pc(inp["final_norm_g"]),
        "sc_cw": _conv_pc(inp["sc_conv_w"][0]), "rg_cw": _conv_pc(inp["rg_conv_w"][0]),
        "rg_cb": _vec_pc(inp["rg_conv_b"][0]), "rg_ba": _vec_pc(inp["rg_b_gate_a"][0]),
        "rg_bx": _vec_pc(inp["rg_b_gate_x"][0]), "rg_lam": _vec_pc(inp["rg_lambda"][0]),
        "ffn_cw0": _conv_pc(inp["ffn_conv_w"][0]), "ffn_cw1": _conv_pc(inp["ffn_conv_w"][1]),
    }
    return np.ascontiguousarray(np.concatenate([parts[n] for n, _ in PP_FIELDS], axis=1))


def tile_pieces():
    pcs = [("scin", jp) for jp in range(4)] + [("scout", h) for h in range(2)]
    pcs += [("up", 0, jp) for jp in range(11)] + [("down", 0, q) for q in range(4)]
    pcs += [("rgin_r", q) for q in range(3)] + [("rgin_g", q) for q in range(3)]
    pcs += [("rgout", h) for h in range(2)]
    pcs += [("up", 1, jp) for jp in range(11)] + [("down", 1, q) for q in range(4)]
    return pcs


def build(mode="fused"):
    nc = bass.Bass("TRN2", target_bir_lowering=False)

    def din(name, shape):
        return nc.dram_tensor(name, list(shape), F32, kind="ExternalInput").ap()

    xt = din("xt", (NR, D, N))
    pp_d = din("pp", (128, NPP))
    sel_d = din("sel", (128, NR * 8))
    w_scin = din("sc_w_in", (D, 3 * D))
    w_scout = din("sc_w_out", (D, D))
    w_rgin = din("rg_w_in", (D, 2 * DRNN))
    w_rgout = din("rg_w_out", (DRNN, D))
    w_ga = din("rg_w_gate_a", (NRB, 128, 128))
    w_gx = din("rg_w_gate_x", (NRB, 128, 128))
    w_up = [din("ffn_w_up%d" % l, (D, 2 * DFF)) for l in range(2)]
    w_down = [din("ffn_w_down%d" % l, (DFF, D)) for l in range(2)]
    inb = [nc.dram_tensor("lp_inb%d" % r, [128, 20], F32) for r in range(NR)]
    outb = [nc.dram_tensor("lp_outb%d" % r, [4 * 128, 20], F32) for r in range(NR)]
    out_d = nc.dram_tensor("out", [NR, D, TNEW], F32, kind="ExternalOutput").ap()

    st = contextlib.ExitStack()
    with st:
        def sb(name, shape, dt):
            return st.enter_context(nc.sbuf_tensor(name, list(shape), dt))

        resid = sb("resid", (128, NCH, N), F32)
        hn = sb("hn", (128, NCH, N), BF16)
        big = sb("big", (128, 20, N), F32)
        tmp = sb("tmp", (128, NTMP, N), F32)
        ring = sb("ring", (128, NSLOT, WSLOT), BF16)
        ppt = sb("ppt", (128, NPP), F32)
        selt = sb("selt", (128, NR * 8), F32)
        wga = sb("wga", (128, NRB, 128), BF16)
        wgx = sb("wgx", (128, NRB, 128), BF16)
        ones = sb("ones", (128, 128), BF16)
        ubt = sb("ubt", (128, 2, N), BF16)
        der = sb("der", (128, 8, NRB), F32)
        lpt = sb("lpt", (128, 2, 20), F32)
        lpall = sb("lpall", (128, 4, 20), F32)
        est = sb("est", (128, 8, NRB), F32)
        psg = [st.enter_context(nc.psum_tensor("psg%d" % i, [128, 2048], F32)) for i in range(2)]
        pview = [psg[i][:, PS_OFF:PS_OFF + N] for i in range(2)]

        P = Prog(nc)
        pc = lambda name, j=0: ppt[:, PP_OFF[name] + j: PP_OFF[name] + j + 1]

        P.dma("sp", "i_pp", lambda e: e.dma_start(out=ppt[:], in_=pp_d), writes=["ppt"])
        P.dma("sp", "i_sel", lambda e: e.dma_start(out=selt[:], in_=sel_d), writes=["selt"])
        P.dma("pool", "i_wga", lambda e: e.dma_start(out=wga[:], in_=w_ga.rearrange("k i j -> i k j")), writes=["wga"])
        P.dma("pool", "i_wgx", lambda e: e.dma_start(out=wgx[:], in_=w_gx.rearrange("k i j -> i k j")), writes=["wgx"])
        P.op("dve", lambda e: e.memset(ones[:], 1.0), writes=["ones"])
        P.op("dve", lambda e: e.memset(est[:], 0.0), writes=["E%d" % i for i in range(7)])
        o_ba, o_bx, o_lam = PP_OFF["rg_ba"], PP_OFF["rg_bx"], PP_OFF["rg_lam"]
        P.op("dve", lambda e: e.tensor_scalar(der[:, 0, :], ppt[:, o_ba:o_ba + NRB], 0.5, None, ALU.mult), reads=["ppt"], writes=["der0"])
        P.op("dve", lambda e: e.tensor_scalar(der[:, 1, :], ppt[:, o_bx:o_bx + NRB], 0.5, None, ALU.mult), reads=["ppt"], writes=["der1"])
        P.op("act", lambda e: e.activation(der[:, 4, :], ppt[:, o_lam:o_lam + NRB], AF.Abs), reads=["ppt"], writes=["der4"])
        P.op("act", lambda e: e.activation(der[:, 4, :], der[:, 4, :], AF.Exp, scale=-1.0), reads=["der4"], writes=["der4"])
        P.op("act", lambda e: e.activation(der[:, 5, :], der[:, 4, :], AF.Ln, bias=1.0), reads=["der4"], writes=["der5"])
        P.op("dve", lambda e: e.tensor_scalar(der[:, 6, :], ppt[:, o_lam:o_lam + NRB], -1.0, 0.0, ALU.mult, ALU.max), reads=["ppt"], writes=["der6"])
        P.op("dve", lambda e: e.tensor_tensor(der[:, 5, :], der[:, 5, :], der[:, 6, :], ALU.add), reads=["der5", "der6"], writes=["der5"])
        P.op("dve", lambda e: e.tensor_scalar(der[:, 2, :], der[:, 5, :], -8.0, None, ALU.mult), reads=["der5"], writes=["der2"])
        P.op("dve", lambda e: e.tensor_scalar(der[:, 3, :], der[:, 5, :], -4.0, None, ALU.mult), reads=["der5"], writes=["der3"])
        dcol = lambda i, j: der[:, i, j:j + 1]

        pieces = tile_pieces() * NR
        wstate = {"issued": 0, "used": 0}

        def piece_dmas(tag):
            kind = tag[0]
            if kind == "scin":
                jp = tag[1]
                v = w_scin.rearrange("(k p) n -> p k n", p=128)
                return [(v[:, :, D + 256 * jp: D + 256 * jp + 256], 0, 8, 256),
                        (v[:, :, 2 * D + 256 * jp: 2 * D + 256 * jp + 256], 2048, 8, 256),
                        (v[:, :, 256 * jp: 256 * jp + 256], 4096, 8, 256)]
            if kind == "scout":
                v = w_scout.rearrange("(k p) n -> p k n", p=128)
                return [(v[:, :, 512 * tag[1]: 512 * tag[1] + 512], 0, 8, 512)]
            if kind == "up":
                l, jp = tag[1], tag[2]
                v = w_up[l].rearrange("(k p) n -> p k n", p=128)
                return [(v[:, :, 256 * jp: 256 * jp + 256], 0, 8, 256),
                        (v[:, :, DFF + 256 * jp: DFF + 256 * jp + 256], 2048, 8, 256)]
            if kind == "down":
                l, q = tag[1], tag[2]
                v = w_down[l].rearrange("(k p) n -> p k n", p=128)
                return [(v[:, :, 256 * q: 256 * q + 256], 0, NFF, 256)]
            if kind in ("rgin_r", "rgin_g"):
                q = tag[1]
                base = (DRNN if kind == "rgin_r" else 0) + 512 * q
                ncols = 512 if q < 2 else 256
                v = w_rgin.rearrange("(k p) n -> p k n", p=128)
                return [(v[:, :, base: base + ncols], 0, 8, ncols)]
            if kind == "rgout":
                v = w_rgout.rearrange("(k p) n -> p k n", p=128)
                return [(v[:, :, 512 * tag[1]: 512 * tag[1] + 512], 0, NRB, 512)]
            raise ValueError(tag)

        def issue_upto(i):
            while wstate["issued"] <= min(i, len(pieces) - 1):
                idx = wstate["issued"]
                slot = idx % NSLOT
                for (src, off, KC, ncols) in piece_dmas(pieces[idx]):
                    dst = ring[:, slot, off: off + KC * ncols].rearrange("p (k n) -> p k n", k=KC)
                    P.dma("pool", "ring%d" % slot,
                          (lambda dst=dst, src=src: lambda e: e.dma_start(out=dst, in_=src))(),
                          writes=[("ring", slot)])
                wstate["issued"] += 1

        def next_piece(tag):
            idx = wstate["used"]
            assert pieces[idx] == tag, (pieces[idx], tag)
            issue_upto(idx + NSLOT - 1)
            wstate["used"] += 1
            slot = idx % NSLOT

            def view(off, KC, ncols):
                return ring[:, slot, off: off + KC * ncols].rearrange("p (k n) -> p k n", k=KC)
            return slot, view

        gstate = {"g": 0}

        def nextg():
            g = gstate["g"]
            gstate["g"] ^= 1
            return g

        tstate = {"t": 0, "skip": None}

        def newtmp():
            t = tstate["t"]
            if t == tstate["skip"]:
                t = (t + 1) % NTMP
            tstate["t"] = (t + 1) % NTMP
            return tmp[:, t, :], ("tmp", t)

        def mm_block(g, KC, lhs_fn, rhs_fn, wreads, kkeys, split=False):
            def mk(k0, k1):
                def fn(e):
                    last = None
                    for kc in range(k0, k1):
                        for (s0, s1) in SEGS:
                            last = e.matmul(pview[g][:, s0:s1], lhs_fn(kc), rhs_fn(kc, s0, s1),
                                            start=(kc == 0), stop=(kc == KC - 1))
                    return last
                return fn
            if split:
                for kc in range(KC):
                    P.op("pe", mk(kc, kc + 1), reads=list(wreads) + [kkeys[kc]], writes=[("ps", g)])
            else:
                P.op("pe", mk(0, KC), reads=list(wreads) + list(dict.fromkeys(kkeys)), writes=[("ps", g)])

        def hn_rhs(kc, s0, s1):
            return hn[:, kc, s0:s1]
        HN_KEYS = [("hn", j) for j in range(NCH)]
        RES_KEYS = [("resid", j) for j in range(NCH)]
        XROWS = list(range(11, 19))
        BU_ROWS = [8, 9, 10, 19]

        def conv_taps(dst, dkey, src, skey, wname, wbase, ntaps):
            for s in range(1, ntaps):
                wcol = pc(wname, wbase + ntaps - 1 - s)
                P.op("dve", (lambda s=s, wcol=wcol: lambda e: e.scalar_tensor_tensor(
                    dst[:, s:N], src[:, 0:N - s], wcol, dst[:, s:N], ALU.mult, ALU.add))(),
                    reads=[skey, dkey, "ppt"], writes=[dkey])

        def bf_half(row, half):
            v = big[:, row, :].bitcast(BF16)
            return v[:, half * N:(half + 1) * N], ("big", row)

        def norm_squares(src_fn, sq_fn):
            for j in range(NCH):
                (sa, sk), (qa, qk) = src_fn(j), sq_fn(j)
                P.op("act", (lambda sa=sa, qa=qa: lambda e: e.activation(qa, sa, AF.Square))(), reads=[sk], writes=[qk])

        def norm_stats(sq_fn):
            g = nextg()
            mm_block(g, NCH, lambda kc: ones[:], lambda kc, s0, s1: sq_fn(kc)[0][:, s0:s1], ["ones"],
                     [sq_fn(kc)[1] for kc in range(NCH)], split=True)
            rs, rkey = newtmp()
            P.op("act", lambda e: e.activation(rs, pview[g], AF.Ln, bias=EPS, scale=1.0 / D),
                 reads=[("ps", g)], writes=[rkey])
            P.op("act", lambda e: e.activation(rs, rs, AF.Exp, scale=-0.5), reads=[rkey], writes=[rkey])
            return rs, rkey

        def norm_apply(src_fn, dst_fn, gname, rs, rkey):
            for j in range(NCH):
                (sa, sk), (da, dk) = src_fn(j), dst_fn(j)
                P.op("dve", (lambda sa=sa, da=da, j=j: lambda e: e.scalar_tensor_tensor(
                    da, sa, pc(gname, j), rs, ALU.mult, ALU.mult))(),
                    reads=[sk, rkey, "ppt"], writes=[dk])

        def preload_ln_table():
            P.op("act", lambda e: e.activation(est[:, 7, 0:1], ones[:, 0:1], AF.Ln), reads=["ones"], writes=["E7"])

        res_fn = lambda j: (resid[:, j, :], ("resid", j))
        hn_fn = lambda j: (hn[:, j, :], ("hn", j))
        xin_fn = lambda j: (big[:, XROWS[j], :], ("big", XROWS[j]))

        def rmsnorm(gname):
            norm_squares(res_fn, hn_fn)
            rs, rkey = norm_stats(hn_fn)
            norm_apply(res_fn, hn_fn, gname, rs, rkey)

        def resid_add(ob, g):
            P.op("dve", lambda e: e.tensor_tensor(resid[:, ob, :], pview[g], resid[:, ob, :], ALU.add),
                 reads=[("ps", g), ("resid", ob)], writes=[("resid", ob)])

        def l0_mixer(hook):
            bu_fn = lambda kc: bf_half(BU_ROWS[kc // 2], kc % 2)
            first = True
            for jp in range(4):
                slot, view = next_piece(("scin", jp))
                cv_, vv_, bv_ = view(0, 8, 256), view(2048, 8, 256), view(4096, 8, 256)
                wr = [("ring", slot)]
                for jj in range(2):
                    j = 2 * jp + jj
                    cs = slice(jj * 128, jj * 128 + 128)
                    gc = nextg()
                    mm_block(gc, 8, (lambda kc, w=cv_, cs=cs: w[:, kc, cs]), hn_rhs, wr, HN_KEYS, split=first)
                    csb, ckey = newtmp()
                    P.op("act", (lambda csb=csb, gc=gc: lambda e: e.activation(csb, pview[gc], AF.Copy))(),
                         reads=[("ps", gc)], writes=[ckey])
                    gv = nextg()
                    mm_block(gv, 8, (lambda kc, w=vv_, cs=cs: w[:, kc, cs]), hn_rhs, wr, HN_KEYS)
                    cvt, cvkey = newtmp()
                    P.op("dve", (lambda cvt=cvt, csb=csb, gv=gv: lambda e: e.tensor_tensor(cvt, pview[gv], csb, ALU.mult))(),
                         reads=[("ps", gv), ckey], writes=[cvkey])
                    gb = nextg()
                    mm_block(gb, 8, (lambda kc, w=bv_, cs=cs: w[:, kc, cs]), hn_rhs, wr, HN_KEYS)
                    first = False
                    ut, ukey = newtmp()
                    P.op("act", (lambda ut=ut, cvt=cvt, j=j: lambda e: e.activation(ut, cvt, AF.Identity, scale=pc("sc_cw", 3 * j + 2)))(),
                         reads=[cvkey, "ppt"], writes=[ukey])
                    conv_taps(ut, ukey, cvt, cvkey, "sc_cw", 3 * j, 3)
                    bu, bkey = bu_fn(j)
                    P.op("dve", (lambda bu=bu, ut=ut, gb=gb: lambda e: e.tensor_tensor(bu, pview[gb], ut, ALU.mult))(),
                         reads=[("ps", gb), ukey], writes=[bkey])
                    hook()
            for h in range(2):
                slot, view = next_piece(("scout", h))
                wv = view(0, 8, 512)
                for o in range(4):
                    ob = 4 * h + o
                    g = nextg()
                    mm_block(g, 8, (lambda kc, wv=wv, o=o: wv[:, kc, o * 128:(o + 1) * 128]),
                             lambda kc, s0, s1: bu_fn(kc)[0][:, s0:s1], [("ring", slot)],
                             [bu_fn(kc)[1] for kc in range(8)], split=(ob == 0))
                    resid_add(ob, g)

        def ffn(l, extra_rows, after_up=None, in_down=None):
            rmsnorm("g_ffn%d" % l)
            cwn = "ffn_cw%d" % l
            xstate = {"i": 0}
            ntmp = NTMP + len(extra_rows)

            def ftmp():
                i = xstate["i"]
                xstate["i"] = (i + 1) % ntmp
                if i < NTMP:
                    return tmp[:, i, :], ("tmp", i)
                return big[:, extra_rows[i - NTMP], :], ("big", extra_rows[i - NTMP])

            act_fn = lambda kc: bf_half(kc // 2, kc % 2)
            first = True
            for jp in range(11):
                slot, view = next_piece(("up", l, jp))
                gvw, vvw = view(0, 8, 256), view(2048, 8, 256)
                wr = [("ring", slot)]
                for jj in range(2):
                    j = 2 * jp + jj
                    cs = slice(jj * 128, jj * 128 + 128)
                    outs = []
                    for (wv, blk) in ((gvw, j), (vvw, NFF + j)):
                        g = nextg()
                        mm_block(g, 8, (lambda kc, w=wv, cs=cs: w[:, kc, cs]), hn_rhs, wr, HN_KEYS, split=first)
                        first = False
                        et, ekey = ftmp()
                        P.op("act", (lambda et=et, g=g, blk=blk: lambda e: e.activation(
                            et, pview[g], AF.Identity, scale=pc(cwn, 3 * blk + 2)))(),
                            reads=[("ps", g), "ppt"], writes=[ekey])
                        conv_taps(et, ekey, pview[g], ("ps", g), cwn, 3 * blk, 3)
                        outs.append((et, ekey))
                    (eg, egk), (ev, evk) = outs
                    P.op("act", (lambda eg=eg: lambda e: e.activation(eg, eg, AF.Silu))(), reads=[egk], writes=[egk])
                    av, akey = act_fn(j)
                    P.op("dve", (lambda av=av, eg=eg, ev=ev: lambda e: e.tensor_tensor(av, eg, ev, ALU.mult))(),
                         reads=[egk, evk], writes=[akey])
            preload_ln_table()
            if after_up is not None:
                after_up()
            nblk = 0
            for q in range(4):
                slot, view = next_piece(("down", l, q))
                wv = view(0, NFF, 256)
                for o in range(2):
                    ob = 2 * q + o
                    g = nextg()
                    mm_block(g, NFF, (lambda kc, wv=wv, o=o: wv[:, kc, o * 128:(o + 1) * 128]),
                             lambda kc, s0, s1: act_fn(kc)[0][:, s0:s1], [("ring", slot)],
                             [act_fn(kc)[1] for kc in range(NFF)], split=(ob == 0))
                    resid_add(ob, g)
                    nblk += 1
                    if nblk == 2 and in_down is not None:
                        in_down()

        def l1_mixer(r, after_out=None):
            rmsnorm("g_mix1")
            mcol = selt[:, r * 8: r * 8 + 1]
            lps = lpt[:, r % 2, :]
            lpk = ("lpt", r % 2)
            free = [(tmp[:, t, :], ("tmp", t)) for t in range(NTMP)]
            extra_rows = [9, 19, 8, 18]
            free += [(big[:, x, :], ("big", x)) for x in extra_rows]
            retired = set()

            def talloc():
                return free.pop(0)

            def tfree(t):
                if t[1] not in retired:
                    free.append(t)

            def retire_extras(rows):
                for x in rows:
                    k = ("big", x)
                    retired.add(k)
                    for t in list(free):
                        if t[1] == k:
                            free.remove(t)

            pv = {}

            def piece(kind, q):
                if (kind, q) not in pv:
                    slot, view = next_piece((kind, q))
                    pv[(kind, q)] = (slot, view(0, 8, 512 if q < 2 else 256))
                return pv[(kind, q)]

            def wblk(kind, j):
                slot, wv = piece(kind, j // 4)
                o = j % 4
                return (lambda kc, wv=wv, o=o: wv[:, kc, o * 128:(o + 1) * 128]), [("ring", slot)]

            B = [dict() for _ in range(NRB)]
            xbank = [psg[0][:, 1536:2048], psg[1][:, 1536:2048]]
            xstate = {"i": 0}

            def R_pe(j, split=False):
                lf, wr = wblk("rgin_r", j)
                mm_block(j % 2, 8, lf, hn_rhs, wr, HN_KEYS, split=split)

            def R_conv_a(j):
                ut = talloc()
                B[j]["u"] = ut
                G_R = j % 2
                P.op("act", (lambda ut=ut, j=j, G_R=G_R: lambda e: e.activation(
                    ut[0], pview[G_R], AF.Identity, bias=pc("rg_cb", j), scale=pc("rg_cw", 4 * j + 3)))(),
                    reads=[("ps", G_R), "ppt"], writes=[ut[1]])

            def R_conv_b(j):
                ut = B[j]["u"]
                G_R = j % 2
                conv_taps(ut[0], ut[1], pview[G_R], ("ps", G_R), "rg_cw", 4 * j, 4)
                P.op("dve", (lambda ut=ut, j=j: lambda e: e.tensor_copy(ubt[:, j % 2, :], ut[0]))(),
                     reads=[ut[1]], writes=[("ub", j % 2)])

            def R_gates(j):
                at, xt_, a2t = talloc(), talloc(), talloc()
                B[j]["a"], B[j]["x"], B[j]["mh"] = at, xt_, a2t
                for (wt, wk, dst, di) in ((wga, "wga", at, 0), (wgx, "wgx", xt_, 1)):
                    for (s0, s1) in SEGS:
                        xb = xstate["i"]
                        xstate["i"] ^= 1
                        w = s1 - s0
                        P.op("pe", (lambda wt=wt, j=j, xb=xb, s0=s0, s1=s1, w=w: lambda e: e.matmul(
                            xbank[xb][:, 0:w], wt[:, j, :], ubt[:, j % 2, s0:s1], start=True, stop=True))(),
                            reads=[wk, ("ub", j % 2)], writes=[("psx", xb)])
                        P.op("act", (lambda dst=dst, di=di, j=j, xb=xb, s0=s0, s1=s1, w=w: lambda e: e.activation(
                            dst[0][:, s0:s1], xbank[xb][:, 0:w], AF.Tanh, bias=dcol(di, j), scale=0.5))(),
                            reads=[("psx", xb), "der%d" % di], writes=[dst[1]])
                P.op("act", (lambda at=at, a2t=a2t, j=j: lambda e: e.activation(a2t[0], at[0], AF.Exp, bias=dcol(2, j), scale=dcol(2, j)))(),
                     reads=[at[1], "der2"], writes=[a2t[1]])
                P.op("act", (lambda at=at, j=j: lambda e: e.activation(at[0], at[0], AF.Exp, bias=dcol(3, j), scale=dcol(3, j)))(),
                     reads=[at[1], "der3"], writes=[at[1]])
                P.op("act", (lambda a2t=a2t: lambda e: e.activation(a2t[0], a2t[0], AF.Sqrt, bias=0.25, scale=-0.25))(),
                     reads=[a2t[1]], writes=[a2t[1]])

            def R_tail1(j):
                ut, xt_, mh = B[j]["u"], B[j]["x"], B[j]["mh"]
                P.op("dve", lambda e: e.scalar_tensor_tensor(xt_[0], xt_[0], 1.0, ut[0], ALU.add, ALU.mult),
                     reads=[xt_[1], ut[1]], writes=[xt_[1]])
                P.op("dve", lambda e: e.tensor_scalar(xt_[0][:, 0:SCAN0], xt_[0][:, 0:SCAN0], mcol, None, ALU.mult),
                     reads=[xt_[1], "selt"], writes=[xt_[1]])
                P.op("pool", lambda e: e.tensor_tensor(xt_[0], xt_[0], mh[0], ALU.mult),
                     reads=[xt_[1], mh[1]], writes=[xt_[1]])
                tfree(ut)
                tfree(mh)

            def R_tail2(j):
                at, xt_ = B[j]["a"], B[j]["x"]
                hrow, arow = big[:, j, :], big[:, NRB + j, :]
                hk, ak = ("big", j), ("big", NRB + j)
                P.op("dve", lambda e: e.memset(arow[:, 0:SCAN0], 0.0), writes=[ak])
                P.op("dve", lambda e: e.tensor_tensor_scan(arow[:, SCAN0:N], at[0][:, SCAN0:N], at[0][:, SCAN0:N], 1.0, ALU.mult, ALU.min),
                     reads=[at[1]], writes=[ak])
                P.op("dve", lambda e: e.tensor_tensor_scan(hrow, at[0], xt_[0], 0.0, ALU.mult, ALU.add),
                     reads=[at[1], xt_[1]], writes=[hk])
                P.op("dve", lambda e: e.tensor_copy(lps[:, NRB + j:NRB + j + 1], arow[:, LPCOL:LPCOL + 1]), reads=[ak], writes=[lpk])
                P.op("dve", lambda e: e.tensor_copy(lps[:, j:j + 1], hrow[:, LPCOL:LPCOL + 1]), reads=[hk], writes=[lpk])
                tfree(at)
                tfree(xt_)

            R_pe(0, split=True)
            R_conv_a(0)
            R_conv_b(0)
            pending = None
            for j in range(NRB):
                if j == 8:
                    retire_extras([8, 18])
                if j == 9:
                    retire_extras([9, 19])
                if j + 1 < NRB:
                    R_pe(j + 1)
                    R_conv_a(j + 1)
                R_gates(j)
                if j + 1 < NRB:
                    R_conv_b(j + 1)
                R_tail1(j)
                if pending is not None:
                    R_tail2(pending)
                    pending = None
                if j <= 6:
                    pending = j
                else:
                    R_tail2(j)

            P.dma("sp", "lpb", lambda e: e.dma_start(out=inb[r][:, :], in_=lps), reads=[lpk], writes=[("inb", r)])
            P.coll("cc", lambda e: e.collective_compute(
                "AllGather", ALU.bypass, replica_groups=[[0, 1, 2, 3], [4, 5, 6, 7]],
                ins=[inb[r].ap().opt()], outs=[outb[r].ap().opt()]),
                reads=[("inb", r)], writes=[("outb", r)])
            P.dma("sp", "lpi", lambda e: e.dma_start(out=lpall[:], in_=outb[r].ap().rearrange("(k p) c -> p k c", p=128)),
                  reads=[("outb", r)], writes=["lpall"])

            def carry_chain():
                E = lambda i: est[:, i, :]
                P.op("dve", lambda e: e.tensor_scalar(E(5), E(4), selt[:, r * 8 + 1: r * 8 + 2], None, ALU.mult),
                     reads=["E4", "selt"], writes=["E5"])
                prev, prevk = E(4), "E4"
                for k in range(4):
                    P.op("dve", (lambda k=k, prev=prev: lambda e: e.tensor_tensor(E(6), lpall[:, k, NRB:2 * NRB], prev, ALU.mult))(),
                         reads=["lpall", prevk], writes=["E6"])
                    P.op("dve", (lambda k=k: lambda e: e.tensor_tensor(E(k), E(6), lpall[:, k, 0:NRB], ALU.add))(),
                         reads=["lpall", "E6"], writes=["E%d" % k])
                    if k < 3:
                        P.op("dve", (lambda k=k: lambda e: e.scalar_tensor_tensor(
                            E(5), E(k), selt[:, r * 8 + 2 + k: r * 8 + 3 + k], E(5), ALU.mult, ALU.add))(),
                            reads=["E%d" % k, "E5", "selt"], writes=["E5"])
                    prev, prevk = E(k), "E%d" % k
                P.op("dve", lambda e: e.tensor_copy(E(4), E(3)), reads=["E3"], writes=["E4"])

            y_fn = lambda kc: bf_half(NRB + kc, 0)

            def G_pe(j):
                lf, wr = wblk("rgin_g", j)
                g = nextg()
                mm_block(g, 8, lf, hn_rhs, wr, HN_KEYS)
                gt = talloc()
                B[j]["gate"] = gt
                P.op("act", (lambda gt=gt, g=g: lambda e: e.activation(gt[0], pview[g], AF.Gelu_apprx_tanh))(),
                     reads=[("ps", g)], writes=[gt[1]])

            def G_pre(j):
                gt = B[j]["gate"]
                hrow, arow = big[:, j, :], big[:, NRB + j, :]
                hk, ak = ("big", j), ("big", NRB + j)
                P.op("dve", lambda e: e.tensor_tensor(hrow, hrow, gt[0], ALU.mult), reads=[hk, gt[1]], writes=[hk])
                P.op("dve", lambda e: e.tensor_tensor(gt[0], arow, gt[0], ALU.mult), reads=[ak, gt[1]], writes=[gt[1]])

            def G_y(j):
                gt = B[j]["gate"]
                hrow = big[:, j, :]
                hk = ("big", j)
                yv, yk = y_fn(j)
                P.op("dve", lambda e: e.scalar_tensor_tensor(yv, gt[0], est[:, 5, j:j + 1], hrow, ALU.mult, ALU.add),
                     reads=[gt[1], hk, "E5"], writes=[yk])
                tfree(gt)

            LAG = 5
            for j in range(NRB):
                if j == LAG:
                    carry_chain()
                    for jj in range(LAG):
                        G_y(jj)
                G_pe(j)
                G_pre(j)
                if j >= LAG:
                    G_y(j)
            preload_ln_table()
            y_keys = [y_fn(kc)[1] for kc in range(NRB)]
            for h in range(2):
                slot, view = next_piece(("rgout", h))
                wv = view(0, NRB, 512)
                for o in range(4):
                    ob = 4 * h + o
                    g = nextg()
                    mm_block(g, NRB, (lambda kc, wv=wv, o=o: wv[:, kc, o * 128:(o + 1) * 128]),
                             lambda kc, s0, s1: y_fn(kc)[0][:, s0:s1], [("ring", slot)], y_keys, split=(ob == 0))
                    resid_add(ob, g)
            if after_out is not None:
                after_out()

        def load_x(r):
            if r == 0:
                for c in range(NCH):
                    P.dma("sp", "xin0_%d" % c, (lambda c=c: lambda e: e.dma_start(
                        out=big[:, XROWS[c], :], in_=xt[0, c * 128:(c + 1) * 128, :]))(),
                        writes=[("big", XROWS[c])])
                return
            P.dma("sp", "xin", (lambda r=r: lambda e: e.dma_start(
                out=big[:, XROWS[0]:XROWS[-1] + 1, :], in_=xt[r].rearrange("(c p) t -> p c t", p=128)))(),
                writes=[("big", x) for x in XROWS])

        def copy_x_step(j):
            src, skey = xin_fn(j)
            if j % 2 == 0:
                P.op("act", lambda e: e.activation(resid[:, j, :], src, AF.Copy), reads=[skey], writes=[("resid", j)])
            else:
                P.op("dve", lambda e: e.tensor_copy(resid[:, j, :], src), reads=[skey], writes=[("resid", j)])

        fin_sq_fn = lambda j: bf_half(j, 0)
        fin_out_fn = lambda j: (big[:, j, :], ("big", j))

        def final_steps(r):
            steps = []
            box = {}
            if r is not None:
                def st0():
                    box["rs"] = norm_stats(fin_sq_fn)
                    tstate["skip"] = box["rs"][1][1]
                steps.append(st0)
            for j in range(NCH):
                def st(j=j):
                    if r is not None:
                        (sa, sk), (da, dk) = res_fn(j), fin_out_fn(j)
                        rs, rkey = box["rs"]
                        P.op("dve", lambda e: e.scalar_tensor_tensor(da, sa, pc("g_fin", j), rs, ALU.mult, ALU.mult),
                             reads=[sk, rkey, "ppt"], writes=[dk])
                    copy_x_step(j)
                steps.append(st)
            if r is not None:
                def stl():
                    tstate["skip"] = None
                    P.dma("sp", "xout", lambda e: e.dma_start(
                        out=out_d[r].rearrange("(c p) t -> p c t", p=128), in_=big[:, 0:NCH, HALO:N]),
                        reads=[("big", j) for j in range(NCH)], writes=["out"])
                steps.append(stl)
            return steps

        load_x(0)
        norm_squares(xin_fn, hn_fn)
        rs0, rk0 = norm_stats(hn_fn)
        norm_apply(xin_fn, hn_fn, "g_mix0", rs0, rk0)
        pend = {"steps": final_steps(None)}
        for r in range(NR):
            def hook():
                for _ in range(2):
                    if pend["steps"]:
                        pend["steps"].pop(0)()
            l0_mixer(hook)
            assert not pend["steps"]
            ffn(0, list(range(11, 20)))
            last = (r == NR - 1)
            l1_mixer(r, after_out=(None if last else (lambda r=r: load_x(r + 1))))
            nxt = {}

            def after_up():
                norm_squares(xin_fn, hn_fn)

            def in_down():
                nxt["rs"] = norm_stats(hn_fn)
                norm_apply(xin_fn, hn_fn, "g_mix0", nxt["rs"][0], nxt["rs"][1])
            if last:
                ffn(1, [19])
            else:
                ffn(1, [19], after_up=after_up, in_down=in_down)
            if last:
                norm_squares(res_fn, hn_fn)
                rs, rkey = norm_stats(hn_fn)
                for j in range(NCH):
                    (sa, sk), (da, dk) = res_fn(j), fin_out_fn(j)
                    P.op("dve", (lambda sa=sa, da=da, j=j: lambda e: e.scalar_tensor_tensor(
                        da, sa, pc("g_fin", j), rs, ALU.mult, ALU.mult))(), reads=[sk, rkey, "ppt"], writes=[dk])
                    P.dma("sp", "xout", (lambda r=r, j=j: lambda e: e.dma_start(
                        out=out_d[r, j * 128:(j + 1) * 128, :], in_=big[:, j, HALO:N]))(),
                        reads=[dk], writes=["out"])
            else:
                norm_squares(res_fn, fin_sq_fn)
                pend["steps"] = final_steps(r)
        P.wait_all("sp", ["out"])
        P.emit()
    return nc


_CACHE = {}


def _get(mode):
    if mode not in _CACHE:
        _CACHE[mode] = build(mode)
    return _CACHE[mode]


def kernel(**inp):
    x = np.asarray(inp["x"], np.float32)
    meta = np.asarray(inp["meta_tokens"], np.float32)
    ncores = 8
    pp = pack_params(inp)
    f32c = lambda a: np.ascontiguousarray(np.asarray(a, np.float32))
    common = {
        "pp": pp,
        "sc_w_in": f32c(inp["sc_w_in"][0]), "sc_w_out": f32c(inp["sc_w_out"][0]),
        "rg_w_in": f32c(inp["rg_w_in"][0]), "rg_w_out": f32c(inp["rg_w_out"][0]),
        "rg_w_gate_a": f32c(inp["rg_w_gate_a"][0]), "rg_w_gate_x": f32c(inp["rg_w_gate_x"][0]),
        "ffn_w_up0": f32c(inp["ffn_w_up"][0]), "ffn_w_up1": f32c(inp["ffn_w_up"][1]),
        "ffn_w_down0": f32c(inp["ffn_w_down"][0]), "ffn_w_down1": f32c(inp["ffn_w_down"][1]),
    }
    in_maps = []
    for c in range(ncores):
        s, k = c // 4, c % 4
        xe = np.concatenate([meta, x[s]], axis=0)
        xt = np.empty((NR, D, N), np.float32)
        sel = np.zeros((128, NR, 8), np.float32)
        for r in range(NR):
            j = 4 * r + k
            xt[r] = xe[TNEW * j: TNEW * j + N].T
            if j == 0:
                sel[:, r, 0] = 1.0
            sel[:, r, 1 + k] = 1.0
        m = dict(common)
        m["xt"] = xt
        m["sel"] = np.ascontiguousarray(sel.reshape(128, NR * 8))
        in_maps.append(m)

    res2 = run_bass_kernel_spmd(_get("fused"), in_maps, core_ids=list(range(ncores)))
    out = np.empty((2, SEQ, D), np.float32)
    for c in range(ncores):
        s, k = c // 4, c % 4
        o = np.asarray(res2.results[c]["out"], np.float32)
        for r in range(NR):
            j = 4 * r + k
            out[s, TNEW * j: TNEW * (j + 1)] = o[r].T
    return out
```

```python
import contextlib
import numpy as np
import concourse.bass as bass
import concourse.mybir as mybir
from concourse.bass_utils import run_bass_kernel_spmd

F32 = mybir.dt.float32
BF16 = mybir.dt.bfloat16
AF = mybir.ActivationFunctionType
ALU = mybir.AluOpType

ENGS = ("pe", "act", "dve", "pool", "sp")

D = 1024
NCH = 8
NMETA = 16
SEQ = 16384
TNEW = 1024
HALO = 16
N = TNEW + HALO
PS_OFF = 248
SEGS = ((0, 512 - PS_OFF), (512 - PS_OFF, 1024 - PS_OFF), (1024 - PS_OFF, N))
NR = 4
DFF = 2816
NFF = 22
DRNN = 1280
NRB = 10
SCAN0 = 14
LPCOL = TNEW + SCAN0 - 1
EPS = 1e-6
WSLOT = 6144
NSLOT = 3
NTMP = 7


class Prog:
    def __init__(self, nc):
        self.nc = nc
        self.ops = {e: [] for e in ENGS}
        self.count = {e: 0 for e in ENGS}
        self.dma_count = {}
        self.known = {e: {} for e in ENGS}
        self.last_w = {}
        self.readers = {}

    def _deps(self, eng, reads, writes):
        need = {}

        def add(tok):
            if tok is not None and need.get(tok[0], 0) < tok[1]:
                need[tok[0]] = tok[1]

        for r in reads:
            add(self.last_w.get(r))
        for w in writes:
            add(self.last_w.get(w))
            for t in self.readers.get(w, ()):
                add(t)
        waits = []
        for k, v in need.items():
            if k == eng and eng == "pe":
                continue
            if self.known[eng].get(k, 0) >= v:
                continue
            self.known[eng][k] = v
            waits.append((k, v))
        return waits

    def _commit(self, tok, reads, writes):
        for r in reads:
            self.readers.setdefault(r, []).append(tok)
        for w in writes:
            self.last_w[w] = tok
            self.readers[w] = []

    def op(self, eng, fn, reads=(), writes=()):
        waits = self._deps(eng, reads, writes)
        self.count[eng] += 1
        tok = (eng, self.count[eng])
        self.ops[eng].append((waits, fn, (eng, 1)))
        self._commit(tok, reads, writes)

    def dma(self, queue, semkey, fn, reads=(), writes=()):
        waits = self._deps(queue, reads, writes)
        self.dma_count[semkey] = self.dma_count.get(semkey, 0) + 16
        tok = (semkey, self.dma_count[semkey])
        self.ops[queue].append((waits, fn, (semkey, 16)))
        self._commit(tok, reads, writes)

    def coll(self, semkey, fn, reads=(), writes=()):
        waits = self._deps("pool", reads, writes)
        self.dma_count[semkey] = self.dma_count.get(semkey, 0) + 1
        tok = (semkey, self.dma_count[semkey])
        self.ops["pool"].append((waits, fn, (semkey, 1)))
        self._commit(tok, reads, writes)

    def wait_all(self, eng, keys):
        waits = self._deps(eng, keys, ())
        self.ops[eng].append((waits, None, None))

    def emit(self):
        nc = self.nc
        semkeys = list(ENGS) + list(self.dma_count)
        with contextlib.ExitStack() as st:
            sems = {k: st.enter_context(nc.semaphore("s_" + str(k))) for k in semkeys}
            block = st.enter_context(nc.Block())

            def replay(e):
                def run(eng):
                    for waits, fn, inc in self.ops[e]:
                        for k, v in waits:
                            eng.wait_ge(sems[k], v)
                        if fn is None:
                            continue
                        ins = fn(eng)
                        if inc is not None:
                            ins.then_inc(sems[inc[0]], inc[1])
                return run

            block.tensor(replay("pe"))
            block.scalar(replay("act"))
            block.vector(replay("dve"))
            block.gpsimd(replay("pool"))
            block.sync(replay("sp"))


PP_FIELDS = (("g_mix0", 8), ("g_ffn0", 8), ("g_mix1", 8), ("g_ffn1", 8), ("g_fin", 8),
             ("sc_cw", 24), ("rg_cw", 40), ("rg_cb", 10), ("rg_ba", 10), ("rg_bx", 10),
             ("rg_lam", 10), ("ffn_cw0", 132), ("ffn_cw1", 132))
PP_OFF = {}
_o = 0
for _n, _w in PP_FIELDS:
    PP_OFF[_n] = _o
    _o += _w
NPP = _o


def _vec_pc(v):
    v = np.asarray(v, np.float32)
    return np.ascontiguousarray(v.reshape(-1, 128).T)


def _conv_pc(w):
    w = np.asarray(w, np.float32)
    K = w.shape[0]
    a = w.reshape(K, -1, 128)
    return np.ascontiguousarray(a.transpose(2, 1, 0).reshape(128, -1))


def pack_params(inp):
    parts = {
        "g_mix0": _vec_pc(inp["norm_mix_g"][0]), "g_ffn0": _vec_pc(inp["norm_ffn_g"][0]),
        "g_mix1": _vec_pc(inp["norm_mix_g"][1]), "g_ffn1": _vec_pc(inp["norm_ffn_g"][1]),
        "g_fin": _vec_pc(inp["final_norm_g"]),
        "sc_cw": _conv_pc(inp["sc_conv_w"][0]), "rg_cw": _conv_pc(inp["rg_conv_w"][0]),
        "rg_cb": _vec_pc(inp["rg_conv_b"][0]), "rg_ba": _vec_pc(inp["rg_b_gate_a"][0]),
        "rg_bx": _vec_pc(inp["rg_b_gate_x"][0]), "rg_lam": _vec_pc(inp["rg_lambda"][0]),
        "ffn_cw0": _conv_pc(inp["ffn_conv_w"][0]), "ffn_cw1": _conv_pc(inp["ffn_conv_w"][1]),
    }
    return np.ascontiguousarray(np.concatenate([parts[n] for n, _ in PP_FIELDS], axis=1))


def tile_pieces():
    pcs = [("scin", jp) for jp in range(4)] + [("scout", h) for h in range(2)]
    pcs += [("up", 0, jp) for jp in range(11)] + [("down", 0, q) for q in range(4)]
    pcs += [("rgin_r", q) for q in range(3)] + [("rgin_g", q) for q in range(3)]
    pcs += [("rgout", h) for h in range(2)]
    pcs += [("up", 1, jp) for jp in range(11)] + [("down", 1, q) for q in range(4)]
    return pcs


def build(mode="fused"):
    nc = bass.Bass("TRN2", target_bir_lowering=False)

    def din(name, shape):
        return nc.dram_tensor(name, list(shape), F32, kind="ExternalInput").ap()

    xt = din("xt", (NR, D, N))
    pp_d = din("pp", (128, NPP))
    sel_d = din("sel", (128, NR * 8))
    w_scin = din("sc_w_in", (D, 3 * D))
    w_scout = din("sc_w_out", (D, D))
    w_rgin = din("rg_w_in", (D, 2 * DRNN))
    w_rgout = din("rg_w_out", (DRNN, D))
    w_ga = din("rg_w_gate_a", (NRB, 128, 128))
    w_gx = din("rg_w_gate_x", (NRB, 128, 128))
    w_up = [din("ffn_w_up%d" % l, (D, 2 * DFF)) for l in range(2)]
    w_down = [din("ffn_w_down%d" % l, (DFF, D)) for l in range(2)]
    inb = [nc.dram_tensor("lp_inb%d" % r, [128, 20], F32) for r in range(NR)]
    outb = [nc.dram_tensor("lp_outb%d" % r, [4 * 128, 20], F32) for r in range(NR)]
    out_d = nc.dram_tensor("out", [NR, D, TNEW], F32, kind="ExternalOutput").ap()

    st = contextlib.ExitStack()
    with st:
        def sb(name, shape, dt):
            return st.enter_context(nc.sbuf_tensor(name, list(shape), dt))

        resid = sb("resid", (128, NCH, N), F32)
        hn = sb("hn", (128, NCH, N), BF16)
        big = sb("big", (128, 20, N), F32)
        tmp = sb("tmp", (128, NTMP, N), F32)
        ring = sb("ring", (128, NSLOT, WSLOT), BF16)
        ppt = sb("ppt", (128, NPP), F32)
        selt = sb("selt", (128, NR * 8), F32)
        wga = sb("wga", (128, NRB, 128), BF16)
        wgx = sb("wgx", (128, NRB, 128), BF16)
        ones = sb("ones", (128, 128), BF16)
        ubt = sb("ubt", (128, 2, N), BF16)
        der = sb("der", (128, 8, NRB), F32)
        lpt = sb("lpt", (128, 2, 20), F32)
        lpall = sb("lpall", (128, 4, 20), F32)
        est = sb("est", (128, 8, NRB), F32)
        psg = [st.enter_context(nc.psum_tensor("psg%d" % i, [128, 2048], F32)) for i in range(2)]
        pview = [psg[i][:, PS_OFF:PS_OFF + N] for i in range(2)]

        P = Prog(nc)
        pc = lambda name, j=0: ppt[:, PP_OFF[name] + j: PP_OFF[name] + j + 1]

        P.dma("sp", "i_pp", lambda e: e.dma_start(out=ppt[:], in_=pp_d), writes=["ppt"])
        P.dma("sp", "i_sel", lambda e: e.dma_start(out=selt[:], in_=sel_d), writes=["selt"])
        P.dma("pool", "i_wga", lambda e: e.dma_start(out=wga[:], in_=w_ga.rearrange("k i j -> i k j")), writes=["wga"])
        P.dma("pool", "i_wgx", lambda e: e.dma_start(out=wgx[:], in_=w_gx.rearrange("k i j -> i k j")), writes=["wgx"])
        P.op("dve", lambda e: e.memset(ones[:], 1.0), writes=["ones"])
        P.op("dve", lambda e: e.memset(est[:], 0.0), writes=["E%d" % i for i in range(7)])
        o_ba, o_bx, o_lam = PP_OFF["rg_ba"], PP_OFF["rg_bx"], PP_OFF["rg_lam"]
        P.op("dve", lambda e: e.tensor_scalar(der[:, 0, :], ppt[:, o_ba:o_ba + NRB], 0.5, None, ALU.mult), reads=["ppt"], writes=["der0"])
        P.op("dve", lambda e: e.tensor_scalar(der[:, 1, :], ppt[:, o_bx:o_bx + NRB], 0.5, None, ALU.mult), reads=["ppt"], writes=["der1"])
        P.op("act", lambda e: e.activation(der[:, 4, :], ppt[:, o_lam:o_lam + NRB], AF.Abs), reads=["ppt"], writes=["der4"])
        P.op("act", lambda e: e.activation(der[:, 4, :], der[:, 4, :], AF.Exp, scale=-1.0), reads=["der4"], writes=["der4"])
        P.op("act", lambda e: e.activation(der[:, 5, :], der[:, 4, :], AF.Ln, bias=1.0), reads=["der4"], writes=["der5"])
        P.op("dve", lambda e: e.tensor_scalar(der[:, 6, :], ppt[:, o_lam:o_lam + NRB], -1.0, 0.0, ALU.mult, ALU.max), reads=["ppt"], writes=["der6"])
        P.op("dve", lambda e: e.tensor_tensor(der[:, 5, :], der[:, 5, :], der[:, 6, :], ALU.add), reads=["der5", "der6"], writes=["der5"])
        P.op("dve", lambda e: e.tensor_scalar(der[:, 2, :], der[:, 5, :], -8.0, None, ALU.mult), reads=["der5"], writes=["der2"])
        P.op("dve", lambda e: e.tensor_scalar(der[:, 3, :], der[:, 5, :], -4.0, None, ALU.mult), reads=["der5"], writes=["der3"])
        dcol = lambda i, j: der[:, i, j:j + 1]

        pieces = tile_pieces() * NR
        wstate = {"issued": 0, "used": 0}

        def piece_dmas(tag):
            kind = tag[0]
            if kind == "scin":
                jp = tag[1]
                v = w_scin.rearrange("(k p) n -> p k n", p=128)
                return [(v[:, :, D + 256 * jp: D + 256 * jp + 256], 0, 8, 256),
                        (v[:, :, 2 * D + 256 * jp: 2 * D + 256 * jp + 256], 2048, 8, 256),
                        (v[:, :, 256 * jp: 256 * jp + 256], 4096, 8, 256)]
            if kind == "scout":
                v = w_scout.rearrange("(k p) n -> p k n", p=128)
                return [(v[:, :, 512 * tag[1]: 512 * tag[1] + 512], 0, 8, 512)]
            if kind == "up":
                l, jp = tag[1], tag[2]
                v = w_up[l].rearrange("(k p) n -> p k n", p=128)
                return [(v[:, :, 256 * jp: 256 * jp + 256], 0, 8, 256),
                        (v[:, :, DFF + 256 * jp: DFF + 256 * jp + 256], 2048, 8, 256)]
            if kind == "down":
                l, q = tag[1], tag[2]
                v = w_down[l].rearrange("(k p) n -> p k n", p=128)
                return [(v[:, :, 256 * q: 256 * q + 256], 0, NFF, 256)]
            if kind in ("rgin_r", "rgin_g"):
                q = tag[1]
                base = (DRNN if kind == "rgin_r" else 0) + 512 * q
                ncols = 512 if q < 2 else 256
                v = w_rgin.rearrange("(k p) n -> p k n", p=128)
                return [(v[:, :, base: base + ncols], 0, 8, ncols)]
            if kind == "rgout":
                v = w_rgout.rearrange("(k p) n -> p k n", p=128)
                return [(v[:, :, 512 * tag[1]: 512 * tag[1] + 512], 0, NRB, 512)]
            raise ValueError(tag)

        def issue_upto(i):
            while wstate["issued"] <= min(i, len(pieces) - 1):
                idx = wstate["issued"]
                slot = idx % NSLOT
                for (src, off, KC, ncols) in piece_dmas(pieces[idx]):
                    dst = ring[:, slot, off: off + KC * ncols].rearrange("p (k n) -> p k n", k=KC)
                    P.dma("pool", "ring%d" % slot,
                          (lambda dst=dst, src=src: lambda e: e.dma_start(out=dst, in_=src))(),
                          writes=[("ring", slot)])
                wstate["issued"] += 1

        def next_piece(tag):
            idx = wstate["used"]
            assert pieces[idx] == tag, (pieces[idx], tag)
            issue_upto(idx + NSLOT - 1)
            wstate["used"] += 1
            slot = idx % NSLOT

            def view(off, KC, ncols):
                return ring[:, slot, off: off + KC * ncols].rearrange("p (k n) -> p k n", k=KC)
            return slot, view

        gstate = {"g": 0}

        def nextg():
            g = gstate["g"]
            gstate["g"] ^= 1
            return g

        tstate = {"t": 0, "skip": None}

        def newtmp():
            t = tstate["t"]
            if t == tstate["skip"]:
                t = (t + 1) % NTMP
            tstate["t"] = (t + 1) % NTMP
            return tmp[:, t, :], ("tmp", t)

        def mm_block(g, KC, lhs_fn, rhs_fn, wreads, kkeys, split=False):
            def mk(k0, k1):
                def fn(e):
                    last = None
                    for kc in range(k0, k1):
                        for (s0, s1) in SEGS:
                            last = e.matmul(pview[g][:, s0:s1], lhs_fn(kc), rhs_fn(kc, s0, s1),
                                            start=(kc == 0), stop=(kc == KC - 1))
                    return last
                return fn
            if split:
                for kc in range(KC):
                    P.op("pe", mk(kc, kc + 1), reads=list(wreads) + [kkeys[kc]], writes=[("ps", g)])
            else:
                P.op("pe", mk(0, KC), reads=list(wreads) + list(dict.fromkeys(kkeys)), writes=[("ps", g)])

        def hn_rhs(kc, s0, s1):
            return hn[:, kc, s0:s1]
        HN_KEYS = [("hn", j) for j in range(NCH)]
        RES_KEYS = [("resid", j) for j in range(NCH)]
        XROWS = list(range(11, 19))
        BU_ROWS = [8, 9, 10, 19]

        def conv_taps(dst, dkey, src, skey, wname, wbase, ntaps):
            for s in range(1, ntaps):
                wcol = pc(wname, wbase + ntaps - 1 - s)
                P.op("dve", (lambda s=s, wcol=wcol: lambda e: e.scalar_tensor_tensor(
                    dst[:, s:N], src[:, 0:N - s], wcol, dst[:, s:N], ALU.mult, ALU.add))(),
                    reads=[skey, dkey, "ppt"], writes=[dkey])

        def bf_half(row, half):
            v = big[:, row, :].bitcast(BF16)
            return v[:, half * N:(half + 1) * N], ("big", row)

        def norm_squares(src_fn, sq_fn):
            for j in range(NCH):
                (sa, sk), (qa, qk) = src_fn(j), sq_fn(j)
                P.op("act", (lambda sa=sa, qa=qa: lambda e: e.activation(qa, sa, AF.Square))(), reads=[sk], writes=[qk])

        def norm_stats(sq_fn):
            g = nextg()
            mm_block(g, NCH, lambda kc: ones[:], lambda kc, s0, s1: sq_fn(kc)[0][:, s0:s1], ["ones"],
                     [sq_fn(kc)[1] for kc in range(NCH)], split=True)
            rs, rkey = newtmp()
            P.op("act", lambda e: e.activation(rs, pview[g], AF.Ln, bias=EPS, scale=1.0 / D),
                 reads=[("ps", g)], writes=[rkey])
            P.op("act", lambda e: e.activation(rs, rs, AF.Exp, scale=-0.5), reads=[rkey], writes=[rkey])
            return rs, rkey

        def norm_apply(src_fn, dst_fn, gname, rs, rkey):
            for j in range(NCH):
                (sa, sk), (da, dk) = src_fn(j), dst_fn(j)
                P.op("dve", (lambda sa=sa, da=da, j=j: lambda e: e.scalar_tensor_tensor(
                    da, sa, pc(gname, j), rs, ALU.mult, ALU.mult))(),
                    reads=[sk, rkey, "ppt"], writes=[dk])

        def preload_ln_table():
            P.op("act", lambda e: e.activation(est[:, 7, 0:1], ones[:, 0:1], AF.Ln), reads=["ones"], writes=["E7"])

        res_fn = lambda j: (resid[:, j, :], ("resid", j))
        hn_fn = lambda j: (hn[:, j, :], ("hn", j))
        xin_fn = lambda j: (big[:, XROWS[j], :], ("big", XROWS[j]))

        def rmsnorm(gname):
            norm_squares(res_fn, hn_fn)
            rs, rkey = norm_stats(hn_fn)
            norm_apply(res_fn, hn_fn, gname, rs, rkey)

        def resid_add(ob, g):
            P.op("dve", lambda e: e.tensor_tensor(resid[:, ob, :], pview[g], resid[:, ob, :], ALU.add),
                 reads=[("ps", g), ("resid", ob)], writes=[("resid", ob)])

        def l0_mixer(hook):
            bu_fn = lambda kc: bf_half(BU_ROWS[kc // 2], kc % 2)
            first = True
            for jp in range(4):
                slot, view = next_piece(("scin", jp))
                cv_, vv_, bv_ = view(0, 8, 256), view(2048, 8, 256), view(4096, 8, 256)
                wr = [("ring", slot)]
                for jj in range(2):
                    j = 2 * jp + jj
                    cs = slice(jj * 128, jj * 128 + 128)
                    gc = nextg()
                    mm_block(gc, 8, (lambda kc, w=cv_, cs=cs: w[:, kc, cs]), hn_rhs, wr, HN_KEYS, split=first)
                    csb, ckey = newtmp()
                    P.op("act", (lambda csb=csb, gc=gc: lambda e: e.activation(csb, pview[gc], AF.Copy))(),
                         reads=[("ps", gc)], writes=[ckey])
                    gv = nextg()
                    mm_block(gv, 8, (lambda kc, w=vv_, cs=cs: w[:, kc, cs]), hn_rhs, wr, HN_KEYS)
                    cvt, cvkey = newtmp()
                    P.op("dve", (lambda cvt=cvt, csb=csb, gv=gv: lambda e: e.tensor_tensor(cvt, pview[gv], csb, ALU.mult))(),
                         reads=[("ps", gv), ckey], writes=[cvkey])
                    gb = nextg()
                    mm_block(gb, 8, (lambda kc, w=bv_, cs=cs: w[:, kc, cs]), hn_rhs, wr, HN_KEYS)
                    first = False
                    ut, ukey = newtmp()
                    P.op("act", (lambda ut=ut, cvt=cvt, j=j: lambda e: e.activation(ut, cvt, AF.Identity, scale=pc("sc_cw", 3 * j + 2)))(),
                         reads=[cvkey, "ppt"], writes=[ukey])
                    conv_taps(ut, ukey, cvt, cvkey, "sc_cw", 3 * j, 3)
                    bu, bkey = bu_fn(j)
                    P.op("dve", (lambda bu=bu, ut=ut, gb=gb: lambda e: e.tensor_tensor(bu, pview[gb], ut, ALU.mult))(),
                         reads=[("ps", gb), ukey], writes=[bkey])
                    hook()
            for h in range(2):
                slot, view = next_piece(("scout", h))
                wv = view(0, 8, 512)
                for o in range(4):
                    ob = 4 * h + o
                    g = nextg()
                    mm_block(g, 8, (lambda kc, wv=wv, o=o: wv[:, kc, o * 128:(o + 1) * 128]),
                             lambda kc, s0, s1: bu_fn(kc)[0][:, s0:s1], [("ring", slot)],
                             [bu_fn(kc)[1] for kc in range(8)], split=(ob == 0))
                    resid_add(ob, g)

        def ffn(l, extra_rows, after_up=None, in_down=None):
            rmsnorm("g_ffn%d" % l)
            cwn = "ffn_cw%d" % l
            xstate = {"i": 0}
            ntmp = NTMP + len(extra_rows)

            def ftmp():
                i = xstate["i"]
                xstate["i"] = (i + 1) % ntmp
                if i < NTMP:
                    return tmp[:, i, :], ("tmp", i)
                return big[:, extra_rows[i - NTMP], :], ("big", extra_rows[i - NTMP])

            act_fn = lambda kc: bf_half(kc // 2, kc % 2)
            first = True
            for jp in range(11):
                slot, view = next_piece(("up", l, jp))
                gvw, vvw = view(0, 8, 256), view(2048, 8, 256)
                wr = [("ring", slot)]
                for jj in range(2):
                    j = 2 * jp + jj
                    cs = slice(jj * 128, jj * 128 + 128)
                    outs = []
                    for (wv, blk) in ((gvw, j), (vvw, NFF + j)):
                        g = nextg()
                        mm_block(g, 8, (lambda kc, w=wv, cs=cs: w[:, kc, cs]), hn_rhs, wr, HN_KEYS, split=first)
                        first = False
                        et, ekey = ftmp()
                        P.op("act", (lambda et=et, g=g, blk=blk: lambda e: e.activation(
                            et, pview[g], AF.Identity, scale=pc(cwn, 3 * blk + 2)))(),
                            reads=[("ps", g), "ppt"], writes=[ekey])
                        conv_taps(et, ekey, pview[g], ("ps", g), cwn, 3 * blk, 3)
                        outs.append((et, ekey))
                    (eg, egk), (ev, evk) = outs
                    P.op("act", (lambda eg=eg: lambda e: e.activation(eg, eg, AF.Silu))(), reads=[egk], writes=[egk])
                    av, akey = act_fn(j)
                    P.op("pool", (lambda av=av, eg=eg, ev=ev: lambda e: e.tensor_tensor(av, eg, ev, ALU.mult))(),
                         reads=[egk, evk], writes=[akey])
            preload_ln_table()
            if after_up is not None:
                after_up()
            nblk = 0
            for q in range(4):
                slot, view = next_piece(("down", l, q))
                wv = view(0, NFF, 256)
                for o in range(2):
                    ob = 2 * q + o
                    g = nextg()
                    mm_block(g, NFF, (lambda kc, wv=wv, o=o: wv[:, kc, o * 128:(o + 1) * 128]),
                             lambda kc, s0, s1: act_fn(kc)[0][:, s0:s1], [("ring", slot)],
                             [act_fn(kc)[1] for kc in range(NFF)], split=(ob == 0))
                    resid_add(ob, g)
                    nblk += 1
                    if nblk == 2 and in_down is not None:
                        in_down()

        def l1_mixer(r, after_out=None):
            rmsnorm("g_mix1")
            mcol = selt[:, r * 8: r * 8 + 1]
            lps = lpt[:, r % 2, :]
            lpk = ("lpt", r % 2)
            free = [(tmp[:, t, :], ("tmp", t)) for t in range(NTMP)]
            extra_rows = [9, 19, 8, 18]
            free += [(big[:, x, :], ("big", x)) for x in extra_rows]
            retired = set()

            def talloc():
                return free.pop(0)

            def tfree(t):
                if t[1] not in retired:
                    free.append(t)

            def retire_extras(rows):
                for x in rows:
                    k = ("big", x)
                    retired.add(k)
                    for t in list(free):
                        if t[1] == k:
                            free.remove(t)

            pv = {}

            def piece(kind, q):
                if (kind, q) not in pv:
                    slot, view = next_piece((kind, q))
                    pv[(kind, q)] = (slot, view(0, 8, 512 if q < 2 else 256))
                return pv[(kind, q)]

            def wblk(kind, j):
                slot, wv = piece(kind, j // 4)
                o = j % 4
                return (lambda kc, wv=wv, o=o: wv[:, kc, o * 128:(o + 1) * 128]), [("ring", slot)]

            B = [dict() for _ in range(NRB)]
            xbank = [psg[0][:, 1536:2048], psg[1][:, 1536:2048]]
            xstate = {"i": 0}

            def R_pe(j, split=False):
                lf, wr = wblk("rgin_r", j)
                mm_block(j % 2, 8, lf, hn_rhs, wr, HN_KEYS, split=split)

            def R_conv_a(j):
                ut = talloc()
                B[j]["u"] = ut
                G_R = j % 2
                P.op("act", (lambda ut=ut, j=j, G_R=G_R: lambda e: e.activation(
                    ut[0], pview[G_R], AF.Identity, bias=pc("rg_cb", j), scale=pc("rg_cw", 4 * j + 3)))(),
                    reads=[("ps", G_R), "ppt"], writes=[ut[1]])

            def R_conv_b(j):
                ut = B[j]["u"]
                G_R = j % 2
                conv_taps(ut[0], ut[1], pview[G_R], ("ps", G_R), "rg_cw", 4 * j, 4)
                P.op("dve", (lambda ut=ut, j=j: lambda e: e.tensor_copy(ubt[:, j % 2, :], ut[0]))(),
                     reads=[ut[1]], writes=[("ub", j % 2)])

            def R_gates(j):
                at, xt_, a2t = talloc(), talloc(), talloc()
                B[j]["a"], B[j]["x"], B[j]["mh"] = at, xt_, a2t
                for (wt, wk, dst, di) in ((wga, "wga", at, 0), (wgx, "wgx", xt_, 1)):
                    for (s0, s1) in SEGS:
                        xb = xstate["i"]
                        xstate["i"] ^= 1
                        w = s1 - s0
                        P.op("pe", (lambda wt=wt, j=j, xb=xb, s0=s0, s1=s1, w=w: lambda e: e.matmul(
                            xbank[xb][:, 0:w], wt[:, j, :], ubt[:, j % 2, s0:s1], start=True, stop=True))(),
                            reads=[wk, ("ub", j % 2)], writes=[("psx", xb)])
                        P.op("act", (lambda dst=dst, di=di, j=j, xb=xb, s0=s0, s1=s1, w=w: lambda e: e.activation(
                            dst[0][:, s0:s1], xbank[xb][:, 0:w], AF.Tanh, bias=dcol(di, j), scale=0.5))(),
                            reads=[("psx", xb), "der%d" % di], writes=[dst[1]])
                P.op("act", (lambda at=at, a2t=a2t, j=j: lambda e: e.activation(a2t[0], at[0], AF.Exp, bias=dcol(2, j), scale=dcol(2, j)))(),
                     reads=[at[1], "der2"], writes=[a2t[1]])
                P.op("act", (lambda at=at, j=j: lambda e: e.activation(at[0], at[0], AF.Exp, bias=dcol(3, j), scale=dcol(3, j)))(),
                     reads=[at[1], "der3"], writes=[at[1]])
                P.op("act", (lambda a2t=a2t: lambda e: e.activation(a2t[0], a2t[0], AF.Sqrt, bias=0.25, scale=-0.25))(),
                     reads=[a2t[1]], writes=[a2t[1]])

            def R_tail1(j):
                ut, xt_, mh = B[j]["u"], B[j]["x"], B[j]["mh"]
                P.op("dve", lambda e: e.scalar_tensor_tensor(xt_[0], xt_[0], 1.0, ut[0], ALU.add, ALU.mult),
                     reads=[xt_[1], ut[1]], writes=[xt_[1]])
                P.op("dve", lambda e: e.tensor_scalar(xt_[0][:, 0:SCAN0], xt_[0][:, 0:SCAN0], mcol, None, ALU.mult),
                     reads=[xt_[1], "selt"], writes=[xt_[1]])
                P.op("pool", lambda e: e.tensor_tensor(xt_[0], xt_[0], mh[0], ALU.mult),
                     reads=[xt_[1], mh[1]], writes=[xt_[1]])
                tfree(ut)
                tfree(mh)

            def R_tail2(j):
                at, xt_ = B[j]["a"], B[j]["x"]
                hrow, arow = big[:, j, :], big[:, NRB + j, :]
                hk, ak = ("big", j), ("big", NRB + j)
                P.op("dve", lambda e: e.memset(arow[:, 0:SCAN0], 0.0), writes=[ak])
                P.op("dve", lambda e: e.tensor_tensor_scan(arow[:, SCAN0:N], at[0][:, SCAN0:N], at[0][:, SCAN0:N], 1.0, ALU.mult, ALU.min),
                     reads=[at[1]], writes=[ak])
                P.op("dve", lambda e: e.tensor_tensor_scan(hrow, at[0], xt_[0], 0.0, ALU.mult, ALU.add),
                     reads=[at[1], xt_[1]], writes=[hk])
                P.op("dve", lambda e: e.tensor_copy(lps[:, NRB + j:NRB + j + 1], arow[:, LPCOL:LPCOL + 1]), reads=[ak], writes=[lpk])
                P.op("dve", lambda e: e.tensor_copy(lps[:, j:j + 1], hrow[:, LPCOL:LPCOL + 1]), reads=[hk], writes=[lpk])
                tfree(at)
                tfree(xt_)

            R_pe(0, split=True)
            R_conv_a(0)
            R_conv_b(0)
            pending = None
            for j in range(NRB):
                if j == 8:
                    retire_extras([8, 18])
                if j == 9:
                    retire_extras([9, 19])
                if j + 1 < NRB:
                    R_pe(j + 1)
                    R_conv_a(j + 1)
                R_gates(j)
                if j + 1 < NRB:
                    R_conv_b(j + 1)
                R_tail1(j)
                if pending is not None:
                    R_tail2(pending)
                    pending = None
                if j <= 6:
                    pending = j
                else:
                    R_tail2(j)

            P.dma("sp", "lpb", lambda e: e.dma_start(out=inb[r][:, :], in_=lps), reads=[lpk], writes=[("inb", r)])
            P.coll("cc", lambda e: e.collective_compute(
                "AllGather", ALU.bypass, replica_groups=[[0, 1, 2, 3], [4, 5, 6, 7]],
                ins=[inb[r].ap().opt()], outs=[outb[r].ap().opt()]),
                reads=[("inb", r)], writes=[("outb", r)])
            P.dma("sp", "lpi", lambda e: e.dma_start(out=lpall[:], in_=outb[r].ap().rearrange("(k p) c -> p k c", p=128)),
                  reads=[("outb", r)], writes=["lpall"])

            def carry_chain():
                E = lambda i: est[:, i, :]
                P.op("dve", lambda e: e.tensor_scalar(E(5), E(4), selt[:, r * 8 + 1: r * 8 + 2], None, ALU.mult),
                     reads=["E4", "selt"], writes=["E5"])
                prev, prevk = E(4), "E4"
                for k in range(4):
                    P.op("dve", (lambda k=k, prev=prev: lambda e: e.tensor_tensor(E(6), lpall[:, k, NRB:2 * NRB], prev, ALU.mult))(),
                         reads=["lpall", prevk], writes=["E6"])
                    P.op("dve", (lambda k=k: lambda e: e.tensor_tensor(E(k), E(6), lpall[:, k, 0:NRB], ALU.add))(),
                         reads=["lpall", "E6"], writes=["E%d" % k])
                    if k < 3:
                        P.op("dve", (lambda k=k: lambda e: e.scalar_tensor_tensor(
                            E(5), E(k), selt[:, r * 8 + 2 + k: r * 8 + 3 + k], E(5), ALU.mult, ALU.add))(),
                            reads=["E%d" % k, "E5", "selt"], writes=["E5"])
                    prev, prevk = E(k), "E%d" % k
                P.op("dve", lambda e: e.tensor_copy(E(4), E(3)), reads=["E3"], writes=["E4"])

            y_fn = lambda kc: bf_half(NRB + kc, 0)

            def G_pe(j):
                lf, wr = wblk("rgin_g", j)
                g = nextg()
                mm_block(g, 8, lf, hn_rhs, wr, HN_KEYS)
                gt = talloc()
                B[j]["gate"] = gt
                P.op("act", (lambda gt=gt, g=g: lambda e: e.activation(gt[0], pview[g], AF.Gelu_apprx_tanh))(),
                     reads=[("ps", g)], writes=[gt[1]])

            def G_pre(j):
                gt = B[j]["gate"]
                hrow, arow = big[:, j, :], big[:, NRB + j, :]
                hk, ak = ("big", j), ("big", NRB + j)
                P.op("dve", lambda e: e.tensor_tensor(hrow, hrow, gt[0], ALU.mult), reads=[hk, gt[1]], writes=[hk])
                P.op("dve", lambda e: e.tensor_tensor(gt[0], arow, gt[0], ALU.mult), reads=[ak, gt[1]], writes=[gt[1]])

            def G_y(j):
                gt = B[j]["gate"]
                hrow = big[:, j, :]
                hk = ("big", j)
                yv, yk = y_fn(j)
                P.op("dve", lambda e: e.scalar_tensor_tensor(yv, gt[0], est[:, 5, j:j + 1], hrow, ALU.mult, ALU.add),
                     reads=[gt[1], hk, "E5"], writes=[yk])
                tfree(gt)

            LAG = 5
            for j in range(NRB + LAG):
                if j == LAG:
                    carry_chain()
                if j >= LAG:
                    G_y(j - LAG)
                if j < NRB:
                    G_pe(j)
                    G_pre(j)
            preload_ln_table()
            y_keys = [y_fn(kc)[1] for kc in range(NRB)]
            for h in range(2):
                slot, view = next_piece(("rgout", h))
                wv = view(0, NRB, 512)
                for o in range(4):
                    ob = 4 * h + o
                    g = nextg()
                    mm_block(g, NRB, (lambda kc, wv=wv, o=o: wv[:, kc, o * 128:(o + 1) * 128]),
                             lambda kc, s0, s1: y_fn(kc)[0][:, s0:s1], [("ring", slot)], y_keys, split=(ob == 0))
                    resid_add(ob, g)
            if after_out is not None:
                after_out()

        def load_x(r):
            if r == 0:
                for c in range(NCH):
                    P.dma("sp", "xin0_%d" % c, (lambda c=c: lambda e: e.dma_start(
                        out=big[:, XROWS[c], :], in_=xt[0, c * 128:(c + 1) * 128, :]))(),
                        writes=[("big", XROWS[c])])
                return
            P.dma("sp", "xin", (lambda r=r: lambda e: e.dma_start(
                out=big[:, XROWS[0]:XROWS[-1] + 1, :], in_=xt[r].rearrange("(c p) t -> p c t", p=128)))(),
                writes=[("big", x) for x in XROWS])

        def copy_x_step(j):
            src, skey = xin_fn(j)
            if j % 2 == 0:
                P.op("act", lambda e: e.activation(resid[:, j, :], src, AF.Copy), reads=[skey], writes=[("resid", j)])
            else:
                P.op("dve", lambda e: e.tensor_copy(resid[:, j, :], src), reads=[skey], writes=[("resid", j)])

        fin_sq_fn = lambda j: bf_half(j, 0)
        fin_out_fn = lambda j: (big[:, j, :], ("big", j))

        def final_steps(r):
            steps = []
            box = {}
            if r is not None:
                def st0():
                    box["rs"] = norm_stats(fin_sq_fn)
                    tstate["skip"] = box["rs"][1][1]
                steps.append(st0)
            for j in range(NCH):
                def st(j=j):
                    if r is not None:
                        (sa, sk), (da, dk) = res_fn(j), fin_out_fn(j)
                        rs, rkey = box["rs"]
                        P.op("dve", lambda e: e.scalar_tensor_tensor(da, sa, pc("g_fin", j), rs, ALU.mult, ALU.mult),
                             reads=[sk, rkey, "ppt"], writes=[dk])
                    copy_x_step(j)
                steps.append(st)
            if r is not None:
                def stl():
                    tstate["skip"] = None
                    P.dma("sp", "xout", lambda e: e.dma_start(
                        out=out_d[r].rearrange("(c p) t -> p c t", p=128), in_=big[:, 0:NCH, HALO:N]),
                        reads=[("big", j) for j in range(NCH)], writes=["out"])
                steps.append(stl)
            return steps

        load_x(0)
        norm_squares(xin_fn, hn_fn)
        rs0, rk0 = norm_stats(hn_fn)
        norm_apply(xin_fn, hn_fn, "g_mix0", rs0, rk0)
        pend = {"steps": final_steps(None)}
        for r in range(NR):
            def hook():
                for _ in range(2):
                    if pend["steps"]:
                        pend["steps"].pop(0)()
            l0_mixer(hook)
            assert not pend["steps"]
            ffn(0, list(range(11, 20)))
            last = (r == NR - 1)
            l1_mixer(r, after_out=(None if last else (lambda r=r: load_x(r + 1))))
            nxt = {}

            def after_up():
                norm_squares(xin_fn, hn_fn)

            def in_down():
                nxt["rs"] = norm_stats(hn_fn)
                norm_apply(xin_fn, hn_fn, "g_mix0", nxt["rs"][0], nxt["rs"][1])
            if last:
                ffn(1, [19])
            else:
                ffn(1, [19], after_up=after_up, in_down=in_down)
            if last:
                norm_squares(res_fn, hn_fn)
                rs, rkey = norm_stats(hn_fn)
                for j in range(NCH):
                    (sa, sk), (da, dk) = res_fn(j), fin_out_fn(j)
                    P.op("dve", (lambda sa=sa, da=da, j=j: lambda e: e.scalar_tensor_tensor(
                        da, sa, pc("g_fin", j), rs, ALU.mult, ALU.mult))(), reads=[sk, rkey, "ppt"], writes=[dk])
                    P.dma("sp", "xout", (lambda r=r, j=j: lambda e: e.dma_start(
                        out=out_d[r, j * 128:(j + 1) * 128, :], in_=big[:, j, HALO:N]))(),
                        reads=[dk], writes=["out"])
            else:
                norm_squares(res_fn, fin_sq_fn)
                pend["steps"] = final_steps(r)
        P.wait_all("sp", ["out"])
        P.emit()
    return nc


_CACHE = {}


def _get(mode):
    if mode not in _CACHE:
        _CACHE[mode] = build(mode)
    return _CACHE[mode]


def kernel(**inp):
    x = np.asarray(inp["x"], np.float32)
    meta = np.asarray(inp["meta_tokens"], np.float32)
    ncores = 8
    pp = pack_params(inp)
    f32c = lambda a: np.ascontiguousarray(np.asarray(a, np.float32))
    common = {
        "pp": pp,
        "sc_w_in": f32c(inp["sc_w_in"][0]), "sc_w_out": f32c(inp["sc_w_out"][0]),
        "rg_w_in": f32c(inp["rg_w_in"][0]), "rg_w_out": f32c(inp["rg_w_out"][0]),
        "rg_w_gate_a": f32c(inp["rg_w_gate_a"][0]), "rg_w_gate_x": f32c(inp["rg_w_gate_x"][0]),
        "ffn_w_up0": f32c(inp["ffn_w_up"][0]), "ffn_w_up1": f32c(inp["ffn_w_up"][1]),
        "ffn_w_down0": f32c(inp["ffn_w_down"][0]), "ffn_w_down1": f32c(inp["ffn_w_down"][1]),
    }
    in_maps = []
    for c in range(ncores):
        s, k = c // 4, c % 4
        xe = np.concatenate([meta, x[s]], axis=0)
        xt = np.empty((NR, D, N), np.float32)
        sel = np.zeros((128, NR, 8), np.float32)
        for r in range(NR):
            j = 4 * r + k
            xt[r] = xe[TNEW * j: TNEW * j + N].T
            if j == 0:
                sel[:, r, 0] = 1.0
            sel[:, r, 1 + k] = 1.0
        m = dict(common)
        m["xt"] = xt
        m["sel"] = np.ascontiguousarray(sel.reshape(128, NR * 8))
        in_maps.append(m)

    res2 = run_bass_kernel_spmd(_get("fused"), in_maps, core_ids=list(range(ncores)))
    out = np.empty((2, SEQ, D), np.float32)
    for c in range(ncores):
        s, k = c // 4, c % 4
        o = np.asarray(res2.results[c]["out"], np.float32)
        for r in range(NR):
            j = 4 * r + k
            out[s, TNEW * j: TNEW * (j + 1)] = o[r].T
    return out
```

```python
import contextlib
import numpy as np
import concourse.bass as bass
import concourse.mybir as mybir
from concourse.bass_utils import run_bass_kernel_spmd

F32 = mybir.dt.float32
BF16 = mybir.dt.bfloat16
AF = mybir.ActivationFunctionType
ALU = mybir.AluOpType

ENGS = ("pe", "act", "dve", "pool", "sp")

D = 1024
NCH = 8
NMETA = 16
SEQ = 16384
TNEW = 1024
HALO = 16
N = TNEW + HALO
PS_OFF = 256
SEGS = ((0, 512 - PS_OFF), (512 - PS_OFF, 1024 - PS_OFF), (1024 - PS_OFF, N))
NR = 4
DFF = 2816
NFF = 22
DRNN = 1280
NRB = 10
SCAN0 = 14
LPCOL = TNEW + SCAN0 - 1
EPS = 1e-6
WSLOT = 6144
NSLOT = 3
NTMP = 7


class Prog:
    def __init__(self, nc):
        self.nc = nc
        self.ops = {e: [] for e in ENGS}
        self.count = {e: 0 for e in ENGS}
        self.dma_count = {}
        self.known = {e: {} for e in ENGS}
        self.last_w = {}
        self.readers = {}

    def _deps(self, eng, reads, writes):
        need = {}

        def add(tok):
            if tok is not None and need.get(tok[0], 0) < tok[1]:
                need[tok[0]] = tok[1]

        for r in reads:
            add(self.last_w.get(r))
        for w in writes:
            add(self.last_w.get(w))
            for t in self.readers.get(w, ()):
                add(t)
        waits = []
        for k, v in need.items():
            if k == eng and eng == "pe":
                continue
            if self.known[eng].get(k, 0) >= v:
                continue
            self.known[eng][k] = v
            waits.append((k, v))
        return waits

    def _commit(self, tok, reads, writes):
        for r in reads:
            self.readers.setdefault(r, []).append(tok)
        for w in writes:
            self.last_w[w] = tok
            self.readers[w] = []

    def op(self, eng, fn, reads=(), writes=()):
        waits = self._deps(eng, reads, writes)
        self.count[eng] += 1
        tok = (eng, self.count[eng])
        self.ops[eng].append((waits, fn, (eng, 1)))
        self._commit(tok, reads, writes)

    def dma(self, queue, semkey, fn, reads=(), writes=()):
        waits = self._deps(queue, reads, writes)
        self.dma_count[semkey] = self.dma_count.get(semkey, 0) + 16
        tok = (semkey, self.dma_count[semkey])
        self.ops[queue].append((waits, fn, (semkey, 16)))
        self._commit(tok, reads, writes)

    def coll(self, semkey, fn, reads=(), writes=()):
        waits = self._deps("pool", reads, writes)
        self.dma_count[semkey] = self.dma_count.get(semkey, 0) + 1
        tok = (semkey, self.dma_count[semkey])
        self.ops["pool"].append((waits, fn, (semkey, 1)))
        self._commit(tok, reads, writes)

    def wait_all(self, eng, keys):
        waits = self._deps(eng, keys, ())
        self.ops[eng].append((waits, None, None))

    def emit(self):
        nc = self.nc
        semkeys = list(ENGS) + list(self.dma_count)
        with contextlib.ExitStack() as st:
            sems = {k: st.enter_context(nc.semaphore("s_" + str(k))) for k in semkeys}
            block = st.enter_context(nc.Block())

            def replay(e):
                def run(eng):
                    for waits, fn, inc in self.ops[e]:
                        for k, v in waits:
                            eng.wait_ge(sems[k], v)
                        if fn is None:
                            continue
                        ins = fn(eng)
                        if inc is not None:
                            ins.then_inc(sems[inc[0]], inc[1])
                return run

            block.tensor(replay("pe"))
            block.scalar(replay("act"))
            block.vector(replay("dve"))
            block.gpsimd(replay("pool"))
            block.sync(replay("sp"))


PP_FIELDS = (("g_mix0", 8), ("g_ffn0", 8), ("g_mix1", 8), ("g_ffn1", 8), ("g_fin", 8),
             ("sc_cw", 24), ("rg_cw", 40), ("rg_cb", 10), ("rg_ba", 10), ("rg_bx", 10),
             ("rg_lam", 10), ("ffn_cw0", 132), ("ffn_cw1", 132))
PP_OFF = {}
_o = 0
for _n, _w in PP_FIELDS:
    PP_OFF[_n] = _o
    _o += _w
NPP = _o


def _vec_pc(v):
    v = np.asarray(v, np.float32)
    return np.ascontiguousarray(v.reshape(-1, 128).T)


def _conv_pc(w):
    w = np.asarray(w, np.float32)
    K = w.shape[0]
    a = w.reshape(K, -1, 128)
    return np.ascontiguousarray(a.transpose(2, 1, 0).reshape(128, -1))


def pack_params(inp):
    parts = {
        "g_mix0": _vec_pc(inp["norm_mix_g"][0]), "g_ffn0": _vec_pc(inp["norm_ffn_g"][0]),
        "g_mix1": _vec_pc(inp["norm_mix_g"][1]), "g_ffn1": _vec_pc(inp["norm_ffn_g"][1]),
        "g_fin": _vec_pc(inp["final_norm_g"]),
        "sc_cw": _conv_pc(inp["sc_conv_w"][0]), "rg_cw": _conv_pc(inp["rg_conv_w"][0]),
        "rg_cb": _vec_pc(inp["rg_conv_b"][0]), "rg_ba": _vec_pc(inp["rg_b_gate_a"][0]),
        "rg_bx": _vec_pc(inp["rg_b_gate_x"][0]), "rg_lam": _vec_pc(inp["rg_lambda"][0]),
        "ffn_cw0": _conv_pc(inp["ffn_conv_w"][0]), "ffn_cw1": _conv_pc(inp["ffn_conv_w"][1]),
    }
    return np.ascontiguousarray(np.concatenate([parts[n] for n, _ in PP_FIELDS], axis=1))


def tile_pieces():
    pcs = [("scin", jp) for jp in range(4)] + [("scout", h) for h in range(2)]
    pcs += [("up", 0, jp) for jp in range(11)] + [("down", 0, q) for q in range(4)]
    pcs += [("rgin_r", q) for q in range(3)] + [("rgin_g", q) for q in range(3)]
    pcs += [("rgout", h) for h in range(2)]
    pcs += [("up", 1, jp) for jp in range(11)] + [("down", 1, q) for q in range(4)]
    return pcs


def build(mode="fused"):
    nc = bass.Bass("TRN2", target_bir_lowering=False)

    def din(name, shape):
        return nc.dram_tensor(name, list(shape), F32, kind="ExternalInput").ap()

    xt = din("xt", (NR, D, N))
    pp_d = din("pp", (128, NPP))
    sel_d = din("sel", (128, NR * 8))
    w_scin = din("sc_w_in", (D, 3 * D))
    w_scout = din("sc_w_out", (D, D))
    w_rgin = din("rg_w_in", (D, 2 * DRNN))
    w_rgout = din("rg_w_out", (DRNN, D))
    w_ga = din("rg_w_gate_a", (NRB, 128, 128))
    w_gx = din("rg_w_gate_x", (NRB, 128, 128))
    w_up = [din("ffn_w_up%d" % l, (D, 2 * DFF)) for l in range(2)]
    w_down = [din("ffn_w_down%d" % l, (DFF, D)) for l in range(2)]
    inb = [nc.dram_tensor("lp_inb%d" % r, [128, 20], F32) for r in range(NR)]
    outb = [nc.dram_tensor("lp_outb%d" % r, [4 * 128, 20], F32) for r in range(NR)]
    out_d = nc.dram_tensor("out", [NR, D, TNEW], F32, kind="ExternalOutput").ap()

    st = contextlib.ExitStack()
    with st:
        def sb(name, shape, dt):
            return st.enter_context(nc.sbuf_tensor(name, list(shape), dt))

        resid = sb("resid", (128, NCH, N), F32)
        hn = sb("hn", (128, NCH, N), BF16)
        big = sb("big", (128, 20, N), F32)
        tmp = sb("tmp", (128, NTMP, N), F32)
        ring = sb("ring", (128, NSLOT, WSLOT), BF16)
        ppt = sb("ppt", (128, NPP), F32)
        selt = sb("selt", (128, NR * 8), F32)
        wga = sb("wga", (128, NRB, 128), BF16)
        wgx = sb("wgx", (128, NRB, 128), BF16)
        ones = sb("ones", (128, 128), BF16)
        ubt = sb("ubt", (128, 2, N), BF16)
        der = sb("der", (128, 8, NRB), F32)
        lpt = sb("lpt", (128, 2, 20), F32)
        lpall = sb("lpall", (128, 4, 20), F32)
        est = sb("est", (128, 8, NRB), F32)
        psg = [st.enter_context(nc.psum_tensor("psg%d" % i, [128, 2048], F32)) for i in range(2)]
        pview = [psg[i][:, PS_OFF:PS_OFF + N] for i in range(2)]

        P = Prog(nc)
        pc = lambda name, j=0: ppt[:, PP_OFF[name] + j: PP_OFF[name] + j + 1]

        P.dma("sp", "i_pp", lambda e: e.dma_start(out=ppt[:], in_=pp_d), writes=["ppt"])
        P.dma("sp", "i_sel", lambda e: e.dma_start(out=selt[:], in_=sel_d), writes=["selt"])
        P.dma("pool", "i_wga", lambda e: e.dma_start(out=wga[:], in_=w_ga.rearrange("k i j -> i k j")), writes=["wga"])
        P.dma("pool", "i_wgx", lambda e: e.dma_start(out=wgx[:], in_=w_gx.rearrange("k i j -> i k j")), writes=["wgx"])
        P.op("dve", lambda e: e.memset(ones[:], 1.0), writes=["ones"])
        P.op("dve", lambda e: e.memset(est[:], 0.0), writes=["E%d" % i for i in range(7)])
        o_ba, o_bx, o_lam = PP_OFF["rg_ba"], PP_OFF["rg_bx"], PP_OFF["rg_lam"]
        P.op("dve", lambda e: e.tensor_scalar(der[:, 0, :], ppt[:, o_ba:o_ba + NRB], 0.5, None, ALU.mult), reads=["ppt"], writes=["der0"])
        P.op("dve", lambda e: e.tensor_scalar(der[:, 1, :], ppt[:, o_bx:o_bx + NRB], 0.5, None, ALU.mult), reads=["ppt"], writes=["der1"])
        P.op("act", lambda e: e.activation(der[:, 4, :], ppt[:, o_lam:o_lam + NRB], AF.Abs), reads=["ppt"], writes=["der4"])
        P.op("act", lambda e: e.activation(der[:, 4, :], der[:, 4, :], AF.Exp, scale=-1.0), reads=["der4"], writes=["der4"])
        P.op("act", lambda e: e.activation(der[:, 5, :], der[:, 4, :], AF.Ln, bias=1.0), reads=["der4"], writes=["der5"])
        P.op("dve", lambda e: e.tensor_scalar(der[:, 6, :], ppt[:, o_lam:o_lam + NRB], -1.0, 0.0, ALU.mult, ALU.max), reads=["ppt"], writes=["der6"])
        P.op("dve", lambda e: e.tensor_tensor(der[:, 5, :], der[:, 5, :], der[:, 6, :], ALU.add), reads=["der5", "der6"], writes=["der5"])
        P.op("dve", lambda e: e.tensor_scalar(der[:, 2, :], der[:, 5, :], -8.0, None, ALU.mult), reads=["der5"], writes=["der2"])
        P.op("dve", lambda e: e.tensor_scalar(der[:, 3, :], der[:, 5, :], -4.0, None, ALU.mult), reads=["der5"], writes=["der3"])
        dcol = lambda i, j: der[:, i, j:j + 1]

        pieces = tile_pieces() * NR
        wstate = {"issued": 0, "used": 0}

        def piece_dmas(tag):
            kind = tag[0]
            if kind == "scin":
                jp = tag[1]
                v = w_scin.rearrange("(k p) n -> p k n", p=128)
                return [(v[:, :, D + 256 * jp: D + 256 * jp + 256], 0, 8, 256),
                        (v[:, :, 2 * D + 256 * jp: 2 * D + 256 * jp + 256], 2048, 8, 256),
                        (v[:, :, 256 * jp: 256 * jp + 256], 4096, 8, 256)]
            if kind == "scout":
                v = w_scout.rearrange("(k p) n -> p k n", p=128)
                return [(v[:, :, 512 * tag[1]: 512 * tag[1] + 512], 0, 8, 512)]
            if kind == "up":
                l, jp = tag[1], tag[2]
                v = w_up[l].rearrange("(k p) n -> p k n", p=128)
                return [(v[:, :, 256 * jp: 256 * jp + 256], 0, 8, 256),
                        (v[:, :, DFF + 256 * jp: DFF + 256 * jp + 256], 2048, 8, 256)]
            if kind == "down":
                l, q = tag[1], tag[2]
                v = w_down[l].rearrange("(k p) n -> p k n", p=128)
                return [(v[:, :, 256 * q: 256 * q + 256], 0, NFF, 256)]
            if kind in ("rgin_r", "rgin_g"):
                q = tag[1]
                base = (DRNN if kind == "rgin_r" else 0) + 512 * q
                ncols = 512 if q < 2 else 256
                v = w_rgin.rearrange("(k p) n -> p k n", p=128)
                return [(v[:, :, base: base + ncols], 0, 8, ncols)]
            if kind == "rgout":
                v = w_rgout.rearrange("(k p) n -> p k n", p=128)
                return [(v[:, :, 512 * tag[1]: 512 * tag[1] + 512], 0, NRB, 512)]
            raise ValueError(tag)

        def issue_upto(i):
            while wstate["issued"] <= min(i, len(pieces) - 1):
                idx = wstate["issued"]
                slot = idx % NSLOT
                for (src, off, KC, ncols) in piece_dmas(pieces[idx]):
                    dst = ring[:, slot, off: off + KC * ncols].rearrange("p (k n) -> p k n", k=KC)
                    P.dma("pool", "ring%d" % slot,
                          (lambda dst=dst, src=src: lambda e: e.dma_start(out=dst, in_=src))(),
                          writes=[("ring", slot)])
                wstate["issued"] += 1

        def next_piece(tag):
            idx = wstate["used"]
            assert pieces[idx] == tag, (pieces[idx], tag)
            issue_upto(idx + NSLOT - 1)
            wstate["used"] += 1
            slot = idx % NSLOT

            def view(off, KC, ncols):
                return ring[:, slot, off: off + KC * ncols].rearrange("p (k n) -> p k n", k=KC)
            return slot, view

        gstate = {"g": 0}

        def nextg():
            g = gstate["g"]
            gstate["g"] ^= 1
            return g

        tstate = {"t": 0, "skip": None}

        def newtmp():
            t = tstate["t"]
            if t == tstate["skip"]:
                t = (t + 1) % NTMP
            tstate["t"] = (t + 1) % NTMP
            return tmp[:, t, :], ("tmp", t)

        def mm_block(g, KC, lhs_fn, rhs_fn, wreads, kkeys, split=False):
            def mk(k0, k1):
                def fn(e):
                    last = None
                    for kc in range(k0, k1):
                        for (s0, s1) in SEGS:
                            last = e.matmul(pview[g][:, s0:s1], lhs_fn(kc), rhs_fn(kc, s0, s1),
                                            start=(kc == 0), stop=(kc == KC - 1))
                    return last
                return fn
            if split:
                for kc in range(KC):
                    P.op("pe", mk(kc, kc + 1), reads=list(wreads) + [kkeys[kc]], writes=[("ps", g)])
            else:
                P.op("pe", mk(0, KC), reads=list(wreads) + list(dict.fromkeys(kkeys)), writes=[("ps", g)])

        def hn_rhs(kc, s0, s1):
            return hn[:, kc, s0:s1]
        HN_KEYS = [("hn", j) for j in range(NCH)]
        RES_KEYS = [("resid", j) for j in range(NCH)]
        XROWS = list(range(11, 19))
        BU_ROWS = [8, 9, 10, 19]

        def conv_taps(dst, dkey, src, skey, wname, wbase, ntaps):
            for s in range(1, ntaps):
                wcol = pc(wname, wbase + ntaps - 1 - s)
                P.op("dve", (lambda s=s, wcol=wcol: lambda e: e.scalar_tensor_tensor(
                    dst[:, s:N], src[:, 0:N - s], wcol, dst[:, s:N], ALU.mult, ALU.add))(),
                    reads=[skey, dkey, "ppt"], writes=[dkey])

        def bf_half(row, half):
            v = big[:, row, :].bitcast(BF16)
            return v[:, half * N:(half + 1) * N], ("big", row)

        def norm_squares(src_fn, sq_fn):
            for j in range(NCH):
                (sa, sk), (qa, qk) = src_fn(j), sq_fn(j)
                P.op("act", (lambda sa=sa, qa=qa: lambda e: e.activation(qa, sa, AF.Square))(), reads=[sk], writes=[qk])

        def norm_stats(sq_fn):
            g = nextg()
            mm_block(g, NCH, lambda kc: ones[:], lambda kc, s0, s1: sq_fn(kc)[0][:, s0:s1], ["ones"],
                     [sq_fn(kc)[1] for kc in range(NCH)], split=True)
            rs, rkey = newtmp()
            P.op("act", lambda e: e.activation(rs, pview[g], AF.Ln, bias=EPS, scale=1.0 / D),
                 reads=[("ps", g)], writes=[rkey])
            P.op("act", lambda e: e.activation(rs, rs, AF.Exp, scale=-0.5), reads=[rkey], writes=[rkey])
            return rs, rkey

        def norm_apply(src_fn, dst_fn, gname, rs, rkey):
            for j in range(NCH):
                (sa, sk), (da, dk) = src_fn(j), dst_fn(j)
                P.op("dve", (lambda sa=sa, da=da, j=j: lambda e: e.scalar_tensor_tensor(
                    da, sa, pc(gname, j), rs, ALU.mult, ALU.mult))(),
                    reads=[sk, rkey, "ppt"], writes=[dk])

        def preload_ln_table():
            P.op("act", lambda e: e.activation(est[:, 7, 0:1], ones[:, 0:1], AF.Ln), reads=["ones"], writes=["E7"])

        res_fn = lambda j: (resid[:, j, :], ("resid", j))
        hn_fn = lambda j: (hn[:, j, :], ("hn", j))
        xin_fn = lambda j: (big[:, XROWS[j], :], ("big", XROWS[j]))

        def rmsnorm(gname):
            norm_squares(res_fn, hn_fn)
            rs, rkey = norm_stats(hn_fn)
            norm_apply(res_fn, hn_fn, gname, rs, rkey)

        def resid_add(ob, g):
            P.op("dve", lambda e: e.tensor_tensor(resid[:, ob, :], pview[g], resid[:, ob, :], ALU.add),
                 reads=[("ps", g), ("resid", ob)], writes=[("resid", ob)])

        def l0_mixer(hook):
            bu_fn = lambda kc: bf_half(BU_ROWS[kc // 2], kc % 2)
            first = True
            for jp in range(4):
                slot, view = next_piece(("scin", jp))
                cv_, vv_, bv_ = view(0, 8, 256), view(2048, 8, 256), view(4096, 8, 256)
                wr = [("ring", slot)]
                for jj in range(2):
                    j = 2 * jp + jj
                    cs = slice(jj * 128, jj * 128 + 128)
                    gc = nextg()
                    mm_block(gc, 8, (lambda kc, w=cv_, cs=cs: w[:, kc, cs]), hn_rhs, wr, HN_KEYS, split=first)
                    csb, ckey = newtmp()
                    P.op("act", (lambda csb=csb, gc=gc: lambda e: e.activation(csb, pview[gc], AF.Copy))(),
                         reads=[("ps", gc)], writes=[ckey])
                    gv = nextg()
                    mm_block(gv, 8, (lambda kc, w=vv_, cs=cs: w[:, kc, cs]), hn_rhs, wr, HN_KEYS)
                    cvt, cvkey = newtmp()
                    P.op("dve", (lambda cvt=cvt, csb=csb, gv=gv: lambda e: e.tensor_tensor(cvt, pview[gv], csb, ALU.mult))(),
                         reads=[("ps", gv), ckey], writes=[cvkey])
                    gb = nextg()
                    mm_block(gb, 8, (lambda kc, w=bv_, cs=cs: w[:, kc, cs]), hn_rhs, wr, HN_KEYS)
                    first = False
                    ut, ukey = newtmp()
                    P.op("act", (lambda ut=ut, cvt=cvt, j=j: lambda e: e.activation(ut, cvt, AF.Identity, scale=pc("sc_cw", 3 * j + 2)))(),
                         reads=[cvkey, "ppt"], writes=[ukey])
                    conv_taps(ut, ukey, cvt, cvkey, "sc_cw", 3 * j, 3)
                    bu, bkey = bu_fn(j)
                    P.op("dve", (lambda bu=bu, ut=ut, gb=gb: lambda e: e.tensor_tensor(bu, pview[gb], ut, ALU.mult))(),
                         reads=[("ps", gb), ukey], writes=[bkey])
                    hook()
            for h in range(2):
                slot, view = next_piece(("scout", h))
                wv = view(0, 8, 512)
                for o in range(4):
                    ob = 4 * h + o
                    g = nextg()
                    mm_block(g, 8, (lambda kc, wv=wv, o=o: wv[:, kc, o * 128:(o + 1) * 128]),
                             lambda kc, s0, s1: bu_fn(kc)[0][:, s0:s1], [("ring", slot)],
                             [bu_fn(kc)[1] for kc in range(8)], split=(ob == 0))
                    resid_add(ob, g)

        def ffn(l, extra_rows, after_up=None, in_down=None):
            rmsnorm("g_ffn%d" % l)
            cwn = "ffn_cw%d" % l
            xstate = {"i": 0}
            ntmp = NTMP + len(extra_rows)

            def ftmp():
                i = xstate["i"]
                xstate["i"] = (i + 1) % ntmp
                if i < NTMP:
                    return tmp[:, i, :], ("tmp", i)
                return big[:, extra_rows[i - NTMP], :], ("big", extra_rows[i - NTMP])

            act_fn = lambda kc: bf_half(kc // 2, kc % 2)
            first = True
            for jp in range(11):
                slot, view = next_piece(("up", l, jp))
                gvw, vvw = view(0, 8, 256), view(2048, 8, 256)
                wr = [("ring", slot)]
                for jj in range(2):
                    j = 2 * jp + jj
                    cs = slice(jj * 128, jj * 128 + 128)
                    outs = []
                    for (wv, blk) in ((gvw, j), (vvw, NFF + j)):
                        g = nextg()
                        mm_block(g, 8, (lambda kc, w=wv, cs=cs: w[:, kc, cs]), hn_rhs, wr, HN_KEYS, split=first)
                        first = False
                        et, ekey = ftmp()
                        P.op("act", (lambda et=et, g=g, blk=blk: lambda e: e.activation(
                            et, pview[g], AF.Identity, scale=pc(cwn, 3 * blk + 2)))(),
                            reads=[("ps", g), "ppt"], writes=[ekey])
                        conv_taps(et, ekey, pview[g], ("ps", g), cwn, 3 * blk, 3)
                        outs.append((et, ekey))
                    (eg, egk), (ev, evk) = outs
                    P.op("act", (lambda eg=eg: lambda e: e.activation(eg, eg, AF.Silu))(), reads=[egk], writes=[egk])
                    av, akey = act_fn(j)
                    P.op("dve", (lambda av=av, eg=eg, ev=ev: lambda e: e.tensor_tensor(av, eg, ev, ALU.mult))(),
                         reads=[egk, evk], writes=[akey])
            preload_ln_table()
            if after_up is not None:
                after_up()
            nblk = 0
            for q in range(4):
                slot, view = next_piece(("down", l, q))
                wv = view(0, NFF, 256)
                for o in range(2):
                    ob = 2 * q + o
                    g = nextg()
                    mm_block(g, NFF, (lambda kc, wv=wv, o=o: wv[:, kc, o * 128:(o + 1) * 128]),
                             lambda kc, s0, s1: act_fn(kc)[0][:, s0:s1], [("ring", slot)],
                             [act_fn(kc)[1] for kc in range(NFF)], split=(ob == 0))
                    resid_add(ob, g)
                    nblk += 1
                    if nblk == 2 and in_down is not None:
                        in_down()

        def l1_mixer(r, after_out=None):
            rmsnorm("g_mix1")
            mcol = selt[:, r * 8: r * 8 + 1]
            lps = lpt[:, r % 2, :]
            lpk = ("lpt", r % 2)
            free = [(tmp[:, t, :], ("tmp", t)) for t in range(NTMP)]
            extra_rows = [9, 19, 8, 18]
            free += [(big[:, x, :], ("big", x)) for x in extra_rows]
            retired = set()

            def talloc():
                return free.pop(0)

            def tfree(t):
                if t[1] not in retired:
                    free.append(t)

            def retire_extras(rows):
                for x in rows:
                    k = ("big", x)
                    retired.add(k)
                    for t in list(free):
                        if t[1] == k:
                            free.remove(t)

            pv = {}

            def piece(kind, q):
                if (kind, q) not in pv:
                    slot, view = next_piece((kind, q))
                    pv[(kind, q)] = (slot, view(0, 8, 512 if q < 2 else 256))
                return pv[(kind, q)]

            def wblk(kind, j):
                slot, wv = piece(kind, j // 4)
                o = j % 4
                return (lambda kc, wv=wv, o=o: wv[:, kc, o * 128:(o + 1) * 128]), [("ring", slot)]

            B = [dict() for _ in range(NRB)]
            xbank = [psg[0][:, 1536:2048], psg[1][:, 1536:2048]]
            xstate = {"i": 0}

            def R_pe(j, split=False):
                lf, wr = wblk("rgin_r", j)
                mm_block(j % 2, 8, lf, hn_rhs, wr, HN_KEYS, split=split)

            def R_conv_a(j):
                ut = talloc()
                B[j]["u"] = ut
                G_R = j % 2
                P.op("act", (lambda ut=ut, j=j, G_R=G_R: lambda e: e.activation(
                    ut[0], pview[G_R], AF.Identity, bias=pc("rg_cb", j), scale=pc("rg_cw", 4 * j + 3)))(),
                    reads=[("ps", G_R), "ppt"], writes=[ut[1]])

            def R_conv_b(j):
                ut = B[j]["u"]
                G_R = j % 2
                conv_taps(ut[0], ut[1], pview[G_R], ("ps", G_R), "rg_cw", 4 * j, 4)
                P.op("dve", (lambda ut=ut, j=j: lambda e: e.tensor_copy(ubt[:, j % 2, :], ut[0]))(),
                     reads=[ut[1]], writes=[("ub", j % 2)])

            def R_gates(j):
                at, xt_, a2t = talloc(), talloc(), talloc()
                B[j]["a"], B[j]["x"], B[j]["mh"] = at, xt_, a2t
                for (wt, wk, dst, di) in ((wga, "wga", at, 0), (wgx, "wgx", xt_, 1)):
                    for (s0, s1) in SEGS:
                        xb = xstate["i"]
                        xstate["i"] ^= 1
                        w = s1 - s0
                        P.op("pe", (lambda wt=wt, j=j, xb=xb, s0=s0, s1=s1, w=w: lambda e: e.matmul(
                            xbank[xb][:, 0:w], wt[:, j, :], ubt[:, j % 2, s0:s1], start=True, stop=True))(),
                            reads=[wk, ("ub", j % 2)], writes=[("psx", xb)])
                        P.op("act", (lambda dst=dst, di=di, j=j, xb=xb, s0=s0, s1=s1, w=w: lambda e: e.activation(
                            dst[0][:, s0:s1], xbank[xb][:, 0:w], AF.Tanh, bias=dcol(di, j), scale=0.5))(),
                            reads=[("psx", xb), "der%d" % di], writes=[dst[1]])
                P.op("act", (lambda at=at, a2t=a2t, j=j: lambda e: e.activation(a2t[0], at[0], AF.Exp, bias=dcol(2, j), scale=dcol(2, j)))(),
                     reads=[at[1], "der2"], writes=[a2t[1]])
                P.op("act", (lambda at=at, j=j: lambda e: e.activation(at[0], at[0], AF.Exp, bias=dcol(3, j), scale=dcol(3, j)))(),
                     reads=[at[1], "der3"], writes=[at[1]])
                P.op("act", (lambda a2t=a2t: lambda e: e.activation(a2t[0], a2t[0], AF.Sqrt, bias=0.25, scale=-0.25))(),
                     reads=[a2t[1]], writes=[a2t[1]])

            def R_tail1(j):
                ut, xt_, mh = B[j]["u"], B[j]["x"], B[j]["mh"]
                P.op("dve", lambda e: e.scalar_tensor_tensor(xt_[0], xt_[0], 1.0, ut[0], ALU.add, ALU.mult),
                     reads=[xt_[1], ut[1]], writes=[xt_[1]])
                P.op("dve", lambda e: e.tensor_scalar(xt_[0][:, 0:SCAN0], xt_[0][:, 0:SCAN0], mcol, None, ALU.mult),
                     reads=[xt_[1], "selt"], writes=[xt_[1]])
                P.op("pool", lambda e: e.tensor_tensor(xt_[0], xt_[0], mh[0], ALU.mult),
                     reads=[xt_[1], mh[1]], writes=[xt_[1]])
                tfree(ut)
                tfree(mh)

            def R_tail2(j):
                at, xt_ = B[j]["a"], B[j]["x"]
                hrow, arow = big[:, j, :], big[:, NRB + j, :]
                hk, ak = ("big", j), ("big", NRB + j)
                P.op("dve", lambda e: e.memset(arow[:, 0:SCAN0], 0.0), writes=[ak])
                P.op("dve", lambda e: e.tensor_tensor_scan(arow[:, SCAN0:N], at[0][:, SCAN0:N], at[0][:, SCAN0:N], 1.0, ALU.mult, ALU.min),
                     reads=[at[1]], writes=[ak])
                P.op("dve", lambda e: e.tensor_tensor_scan(hrow, at[0], xt_[0], 0.0, ALU.mult, ALU.add),
                     reads=[at[1], xt_[1]], writes=[hk])
                P.op("dve", lambda e: e.tensor_copy(lps[:, NRB + j:NRB + j + 1], arow[:, LPCOL:LPCOL + 1]), reads=[ak], writes=[lpk])
                P.op("dve", lambda e: e.tensor_copy(lps[:, j:j + 1], hrow[:, LPCOL:LPCOL + 1]), reads=[hk], writes=[lpk])
                tfree(at)
                tfree(xt_)

            R_pe(0, split=True)
            R_conv_a(0)
            R_conv_b(0)
            pending = None
            for j in range(NRB):
                if j == 8:
                    retire_extras([8, 18])
                if j == 9:
                    retire_extras([9, 19])
                if j + 1 < NRB:
                    R_pe(j + 1)
                    R_conv_a(j + 1)
                R_gates(j)
                if j + 1 < NRB:
                    R_conv_b(j + 1)
                R_tail1(j)
                if pending is not None:
                    R_tail2(pending)
                    pending = None
                if j <= 6:
                    pending = j
                else:
                    R_tail2(j)

            P.dma("sp", "lpb", lambda e: e.dma_start(out=inb[r][:, :], in_=lps), reads=[lpk], writes=[("inb", r)])
            P.coll("cc", lambda e: e.collective_compute(
                "AllGather", ALU.bypass, replica_groups=[[0, 1, 2, 3], [4, 5, 6, 7]],
                ins=[inb[r].ap().opt()], outs=[outb[r].ap().opt()]),
                reads=[("inb", r)], writes=[("outb", r)])
            P.dma("sp", "lpi", lambda e: e.dma_start(out=lpall[:], in_=outb[r].ap().rearrange("(k p) c -> p k c", p=128)),
                  reads=[("outb", r)], writes=["lpall"])

            def carry_chain():
                E = lambda i: est[:, i, :]
                P.op("dve", lambda e: e.tensor_scalar(E(5), E(4), selt[:, r * 8 + 1: r * 8 + 2], None, ALU.mult),
                     reads=["E4", "selt"], writes=["E5"])
                prev, prevk = E(4), "E4"
                for k in range(4):
                    P.op("dve", (lambda k=k, prev=prev: lambda e: e.tensor_tensor(E(6), lpall[:, k, NRB:2 * NRB], prev, ALU.mult))(),
                         reads=["lpall", prevk], writes=["E6"])
                    P.op("dve", (lambda k=k: lambda e: e.tensor_tensor(E(k), E(6), lpall[:, k, 0:NRB], ALU.add))(),
                         reads=["lpall", "E6"], writes=["E%d" % k])
                    if k < 3:
                        P.op("dve", (lambda k=k: lambda e: e.scalar_tensor_tensor(
                            E(5), E(k), selt[:, r * 8 + 2 + k: r * 8 + 3 + k], E(5), ALU.mult, ALU.add))(),
                            reads=["E%d" % k, "E5", "selt"], writes=["E5"])
                    prev, prevk = E(k), "E%d" % k
                P.op("dve", lambda e: e.tensor_copy(E(4), E(3)), reads=["E3"], writes=["E4"])

            y_fn = lambda kc: bf_half(NRB + kc, 0)

            def G_pe(j):
                lf, wr = wblk("rgin_g", j)
                g = nextg()
                mm_block(g, 8, lf, hn_rhs, wr, HN_KEYS)
                gt = talloc()
                B[j]["gate"] = gt
                P.op("act", (lambda gt=gt, g=g: lambda e: e.activation(gt[0], pview[g], AF.Gelu_apprx_tanh))(),
                     reads=[("ps", g)], writes=[gt[1]])

            def G_pre(j):
                gt = B[j]["gate"]
                hrow, arow = big[:, j, :], big[:, NRB + j, :]
                hk, ak = ("big", j), ("big", NRB + j)
                P.op("dve", lambda e: e.tensor_tensor(hrow, hrow, gt[0], ALU.mult), reads=[hk, gt[1]], writes=[hk])
                P.op("dve", lambda e: e.tensor_tensor(gt[0], arow, gt[0], ALU.mult), reads=[ak, gt[1]], writes=[gt[1]])

            def G_y(j):
                gt = B[j]["gate"]
                hrow = big[:, j, :]
                hk = ("big", j)
                yv, yk = y_fn(j)
                P.op("dve", lambda e: e.scalar_tensor_tensor(yv, gt[0], est[:, 5, j:j + 1], hrow, ALU.mult, ALU.add),
                     reads=[gt[1], hk, "E5"], writes=[yk])
                tfree(gt)

            LAG = 5
            for j in range(NRB + LAG):
                if j == LAG:
                    carry_chain()
                if j >= LAG:
                    G_y(j - LAG)
                if j < NRB:
                    G_pe(j)
                    G_pre(j)
            preload_ln_table()
            y_keys = [y_fn(kc)[1] for kc in range(NRB)]
            for h in range(2):
                slot, view = next_piece(("rgout", h))
                wv = view(0, NRB, 512)
                for o in range(4):
                    ob = 4 * h + o
                    g = nextg()
                    mm_block(g, NRB, (lambda kc, wv=wv, o=o: wv[:, kc, o * 128:(o + 1) * 128]),
                             lambda kc, s0, s1: y_fn(kc)[0][:, s0:s1], [("ring", slot)], y_keys, split=(ob == 0))
                    resid_add(ob, g)
            if after_out is not None:
                after_out()

        def load_x(r):
            if r == 0:
                for c in range(NCH):
                    P.dma("sp", "xin0_%d" % c, (lambda c=c: lambda e: e.dma_start(
                        out=big[:, XROWS[c], :], in_=xt[0, c * 128:(c + 1) * 128, :]))(),
                        writes=[("big", XROWS[c])])
                return
            P.dma("sp", "xin", (lambda r=r: lambda e: e.dma_start(
                out=big[:, XROWS[0]:XROWS[-1] + 1, :], in_=xt[r].rearrange("(c p) t -> p c t", p=128)))(),
                writes=[("big", x) for x in XROWS])

        def copy_x_step(j):
            src, skey = xin_fn(j)
            if j % 2 == 0:
                P.op("act", lambda e: e.activation(resid[:, j, :], src, AF.Copy), reads=[skey], writes=[("resid", j)])
            else:
                P.op("dve", lambda e: e.tensor_copy(resid[:, j, :], src), reads=[skey], writes=[("resid", j)])

        fin_sq_fn = lambda j: bf_half(j, 0)
        fin_out_fn = lambda j: (big[:, j, :], ("big", j))

        def final_steps(r):
            steps = []
            box = {}
            if r is not None:
                def st0():
                    box["rs"] = norm_stats(fin_sq_fn)
                    tstate["skip"] = box["rs"][1][1]
                steps.append(st0)
            for j in range(NCH):
                def st(j=j):
                    if r is not None:
                        (sa, sk), (da, dk) = res_fn(j), fin_out_fn(j)
                        rs, rkey = box["rs"]
                        P.op("dve", lambda e: e.scalar_tensor_tensor(da, sa, pc("g_fin", j), rs, ALU.mult, ALU.mult),
                             reads=[sk, rkey, "ppt"], writes=[dk])
                    copy_x_step(j)
                steps.append(st)
            if r is not None:
                def stl():
                    tstate["skip"] = None
                    P.dma("sp", "xout", lambda e: e.dma_start(
                        out=out_d[r].rearrange("(c p) t -> p c t", p=128), in_=big[:, 0:NCH, HALO:N]),
                        reads=[("big", j) for j in range(NCH)], writes=["out"])
                steps.append(stl)
            return steps

        load_x(0)
        norm_squares(xin_fn, hn_fn)
        rs0, rk0 = norm_stats(hn_fn)
        norm_apply(xin_fn, hn_fn, "g_mix0", rs0, rk0)
        pend = {"steps": final_steps(None)}
        for r in range(NR):
            def hook():
                for _ in range(2):
                    if pend["steps"]:
                        pend["steps"].pop(0)()
            l0_mixer(hook)
            assert not pend["steps"]
            ffn(0, list(range(11, 20)))
            last = (r == NR - 1)
            l1_mixer(r, after_out=(None if last else (lambda r=r: load_x(r + 1))))
            nxt = {}

            def after_up():
                norm_squares(xin_fn, hn_fn)

            def in_down():
                nxt["rs"] = norm_stats(hn_fn)
                norm_apply(xin_fn, hn_fn, "g_mix0", nxt["rs"][0], nxt["rs"][1])
            if last:
                ffn(1, [19])
            else:
                ffn(1, [19], after_up=after_up, in_down=in_down)
            if last:
                norm_squares(res_fn, hn_fn)
                rs, rkey = norm_stats(hn_fn)
                for j in range(NCH):
                    (sa, sk), (da, dk) = res_fn(j), fin_out_fn(j)
                    P.op("dve", (lambda sa=sa, da=da, j=j: lambda e: e.scalar_tensor_tensor(
                        da, sa, pc("g_fin", j), rs, ALU.mult, ALU.mult))(), reads=[sk, rkey, "ppt"], writes=[dk])
                    P.dma("sp", "xout", (lambda r=r, j=j: lambda e: e.dma_start(
                        out=out_d[r, j * 128:(j + 1) * 128, :], in_=big[:, j, HALO:N]))(),
                        reads=[dk], writes=["out"])
            else:
                norm_squares(res_fn, fin_sq_fn)
                pend["steps"] = final_steps(r)
        P.wait_all("sp", ["out"])
        P.emit()
    return nc


_CACHE = {}


def _get(mode):
    if mode not in _CACHE:
        _CACHE[mode] = build(mode)
    return _CACHE[mode]


def kernel(**inp):
    x = np.asarray(inp["x"], np.float32)
    meta = np.asarray(inp["meta_tokens"], np.float32)
    ncores = 8
    pp = pack_params(inp)
    f32c = lambda a: np.ascontiguousarray(np.asarray(a, np.float32))
    common = {
        "pp": pp,
        "sc_w_in": f32c(inp["sc_w_in"][0]), "sc_w_out": f32c(inp["sc_w_out"][0]),
        "rg_w_in": f32c(inp["rg_w_in"][0]), "rg_w_out": f32c(inp["rg_w_out"][0]),
        "rg_w_gate_a": f32c(inp["rg_w_gate_a"][0]), "rg_w_gate_x": f32c(inp["rg_w_gate_x"][0]),
        "ffn_w_up0": f32c(inp["ffn_w_up"][0]), "ffn_w_up1": f32c(inp["ffn_w_up"][1]),
        "ffn_w_down0": f32c(inp["ffn_w_down"][0]), "ffn_w_down1": f32c(inp["ffn_w_down"][1]),
    }
    in_maps = []
    for c in range(ncores):
        s, k = c // 4, c % 4
        xe = np.concatenate([meta, x[s]], axis=0)
        xt = np.empty((NR, D, N), np.float32)
        sel = np.zeros((128, NR, 8), np.float32)
        for r in range(NR):
            j = 4 * r + k
            xt[r] = xe[TNEW * j: TNEW * j + N].T
            if j == 0:
                sel[:, r, 0] = 1.0
            sel[:, r, 1 + k] = 1.0
        m = dict(common)
        m["xt"] = xt
        m["sel"] = np.ascontiguousarray(sel.reshape(128, NR * 8))
        in_maps.append(m)

    res2 = run_bass_kernel_spmd(_get("fused"), in_maps, core_ids=list(range(ncores)))
    out = np.empty((2, SEQ, D), np.float32)
    for c in range(ncores):
        s, k = c // 4, c % 4
        o = np.asarray(res2.results[c]["out"], np.float32)
        for r in range(NR):
            j = 4 * r + k
            out[s, TNEW * j: TNEW * (j + 1)] = o[r].T
    return out
```
